# Optimizing a Trainium2 kernel written in Bass

```python
import math
import jax, jax.numpy as jnp
from jax import lax
import numpy as np

D_MODEL = 1024
BATCH = 32
SEQ = 256
DEPTH = 2
DEC_BATCH = 2
DEC_SEQ = 1024
PAST_LEN = 256

GRID_W = 64
MIX_W = D_MODEL
N_MIXERS = 4
GROUP_W = MIX_W // N_MIXERS
CONV_GROUPS = 4
GLA_HEADS = 4
GLA_DK = GROUP_W // (2 * GLA_HEADS)
GLA_DV = GROUP_W // GLA_HEADS
GLA_RANK = 16
GLA_CHUNK = 64
GLA_NORMALIZER = 16.0
MLP_CHUNK = 128
MLP_GROUPS = 4
MLP_GW = GROUP_W // MLP_GROUPS
HY_ORDER = 2
HY_BANDS = 8
HY_EMB = 1 + 2 * HY_BANDS
HY_HIDDEN = 64
D_FF = ((8 * D_MODEL + 2) // 3 + 255) // 256 * 256
NORM_EPS = 1e-6

A_COLS = 3 * GROUP_W
B_COLS = 2 * GLA_HEADS * GLA_DK + 2 * GROUP_W + 2 * GLA_RANK
C_COLS = 2 * GROUP_W
D_COLS = (HY_ORDER + 1) * GROUP_W
IN_COLS = A_COLS + B_COLS + C_COLS + D_COLS

kernel_name = 'hybrid_diffusion_parallel_groups_step'


def rmsnorm(x, g):
    xf = x.astype(jnp.float32)
    y = xf * lax.rsqrt(jnp.mean(xf * xf, axis=-1, keepdims=True) + NORM_EPS)
    return (y * g).astype(x.dtype)


def layernorm(x, g):
    xf = x.astype(jnp.float32)
    mu = jnp.mean(xf, axis=-1, keepdims=True)
    var = jnp.mean(jnp.square(xf - mu), axis=-1, keepdims=True)
    return ((xf - mu) * lax.rsqrt(var + NORM_EPS) * g).astype(x.dtype)


def row_conv3(x, w, row_w):
    bsz, L, C = x.shape
    rows = L // row_w
    xp = jnp.pad(x.reshape(bsz, rows, row_w, C), ((0, 0), (0, 0), (1, 1), (0, 0)))
    y = xp[:, :, :-2] * w[0] + xp[:, :, 1:-1] * w[1] + xp[:, :, 2:] * w[2]
    return y.reshape(bsz, L, C)


def short_conv_mixer(p, conv_w, row_w):
    b_gate, c_gate, xa = jnp.split(p, 3, axis=-1)
    return b_gate * row_conv3(c_gate * xa, conv_w, row_w)


def gla_chunked(q, k, v, la, s0):
    bsz, L, H, K = q.shape
    V = v.shape[-1]
    n = L // GLA_CHUNK
    q, k, v, la = (t.astype(jnp.float32).reshape(bsz, n, GLA_CHUNK, H, t.shape[-1]) for t in (q, k, v, la))
    b = jnp.cumsum(la, axis=2)
    b_last = b[:, :, -1]
    lower = jnp.tril(jnp.ones((GLA_CHUNK, GLA_CHUNK), dtype=bool))
    diff = b[:, :, :, None] - b[:, :, None, :]
    decay = jnp.exp(jnp.where(lower[:, :, None, None], diff, -jnp.inf))
    scores = jnp.einsum('bnihk,bnjhk,bnijhk->bnhij', q, k, decay)
    o_intra = jnp.einsum('bnhij,bnjhv->bnihv', scores, v)
    u_chunk = jnp.einsum('bnjhk,bnjhv->bnhkv', k * jnp.exp(b_last[:, :, None] - b), v)
    a_chunk = jnp.exp(b_last)

    def step(s, inp):
        a_c, u_c = inp
        return a_c[..., None] * s + u_c, s

    s_fin, s_start = lax.scan(step, s0.astype(jnp.float32),
                              (jnp.moveaxis(a_chunk, 1, 0), jnp.moveaxis(u_chunk, 1, 0)))
    s_start = jnp.moveaxis(s_start, 0, 1)
    o_inter = jnp.einsum('bnihk,bnhkv->bnihv', q * jnp.exp(b), s_start)
    return (o_intra + o_inter).reshape(bsz, L, H, V), s_fin


def gla_mixer(p, s0, wgk, bgk, gnorm):
    bsz, L, _ = p.shape
    qk = GLA_HEADS * GLA_DK
    q = p[..., :qk].reshape(bsz, L, GLA_HEADS, GLA_DK) * (GLA_DK ** -0.5)
    k = p[..., qk:2 * qk].reshape(bsz, L, GLA_HEADS, GLA_DK)
    v = p[..., 2 * qk:2 * qk + GROUP_W].reshape(bsz, L, GLA_HEADS, GLA_DV)
    g = p[..., 2 * qk + GROUP_W:2 * qk + 2 * GROUP_W]
    gk_low = p[..., 2 * qk + 2 * GROUP_W:].reshape(bsz, L, 2, GLA_RANK)
    la = jax.nn.log_sigmoid(jnp.einsum('bldr,drk->bldk', gk_low, wgk) + bgk) / GLA_NORMALIZER
    la = la.reshape(bsz, L, 2, GLA_HEADS, GLA_DK)
    o_f, s_f = gla_chunked(q, k, v, la[:, :, 0], s0[:, 0])
    o_b, s_b = gla_chunked(q[:, ::-1], k[:, ::-1], v[:, ::-1], la[:, ::-1, 1], s0[:, 1])
    o = rmsnorm(o_f + o_b[:, ::-1], gnorm).reshape(bsz, L, GROUP_W)
    return o * jax.nn.silu(g.astype(jnp.float32)), jnp.stack([s_f, s_b], axis=1)


def chunk_mlp_mixer(p, ln_g, ws, bs):
    bsz, L, _ = p.shape
    u, vv = jnp.split(jax.nn.gelu(p), 2, axis=-1)
    vv = layernorm(vv, ln_g).reshape(bsz, L // MLP_CHUNK, MLP_CHUNK, MLP_GROUPS, MLP_GW)
    sp = jnp.einsum('gij,bnjgc->bnigc', ws, vv) + bs.T[None, None, :, :, None]
    return u * sp.reshape(bsz, L, GROUP_W)


def hyena_filters(L, w1, b1, freq, w2, b2, w3, log_decay):
    pos = jnp.arange(L, dtype=jnp.float32)
    t = pos / max(L - 1, 1)
    omega = 2.0 * math.pi * pos / L
    bands = jnp.linspace(1e-4, HY_BANDS - 1, HY_BANDS, dtype=jnp.float32)
    feat = jnp.concatenate([t[:, None], jnp.cos(omega[:, None] * bands), jnp.sin(omega[:, None] * bands)], axis=-1)
    h = jnp.sin(freq * (feat @ w1 + b1))
    h = jnp.sin(freq * (h @ w2 + b2))
    h = (h @ w3).reshape(L, 2, HY_ORDER, GROUP_W).astype(jnp.float32)
    h = h * jnp.exp(-jnp.exp(log_decay)[None] * t[:, None, None, None])
    h_fwd = h[:, 0]
    h_bwd = h[:, 1] * (pos > 0)[:, None, None]
    return h_fwd, h_bwd


def bidir_long_conv(z, h_f, h_b, bias):
    L = z.shape[1]
    n = 2 * L
    zf = z.astype(jnp.float32)
    y_f = jnp.fft.irfft(jnp.fft.rfft(zf, n=n, axis=1) * jnp.fft.rfft(h_f, n=n, axis=0)[None], n=n, axis=1)[:, :L]
    y_b = jnp.fft.irfft(jnp.fft.rfft(zf[:, ::-1], n=n, axis=1) * jnp.fft.rfft(h_b, n=n, axis=0)[None], n=n, axis=1)[:, :L]
    return y_f + y_b[:, ::-1] + zf * bias


def hyena_mixer(p, row_w, conv_w, w1, b1, freq, w2, b2, w3, log_decay, hbias):
    L = p.shape[1]
    pc = row_conv3(p, conv_w, row_w)
    x1, x2, z = jnp.split(pc, 3, axis=-1)
    h_f, h_b = hyena_filters(L, w1, b1, freq, w2, b2, w3, log_decay)
    gates = (x1, x2)
    for o in range(HY_ORDER):
        z = gates[o] * bidir_long_conv(z, h_f[:, o], h_b[:, o], hbias[o])
    return z


def trunk_layer(x, cvec, s0, row_w, lp):
    mod = jax.nn.silu(cvec) @ lp['w_mod'] + lp['b_mod']
    sh1, sc1, g1, sh2, sc2, g2 = jnp.split(mod[:, None, :], 6, axis=-1)
    h = rmsnorm(x, lp['g_pre1']) * (1.0 + sc1) + sh1
    p = h @ lp['w_in']
    o1, o2, o3 = A_COLS, A_COLS + B_COLS, A_COLS + B_COLS + C_COLS
    ya = short_conv_mixer(p[..., :o1], lp['conv_a'], row_w)
    yb, s_new = gla_mixer(p[..., o1:o2], s0, lp['gla_wgk'], lp['gla_bgk'], lp['gla_gnorm'])
    yc = chunk_mlp_mixer(p[..., o2:o3], lp['mlp_ln'], lp['mlp_ws'], lp['mlp_bs'])
    yd = hyena_mixer(p[..., o3:], row_w, lp['hy_conv'], lp['hy_w1'], lp['hy_b1'], lp['hy_freq'],
                     lp['hy_w2'], lp['hy_b2'], lp['hy_w3'], lp['hy_log_decay'], lp['hy_bias'])
    m = jnp.concatenate([ya, yb, yc, yd], axis=-1) @ lp['w_out']
    x = x + g1 * rmsnorm(m, lp['g_post1'])
    h = rmsnorm(x, lp['g_pre2']) * (1.0 + sc2) + sh2
    f = (jax.nn.silu(h @ lp['ffn_w1']) * (h @ lp['ffn_w3'])) @ lp['ffn_w2']
    x = x + g2 * rmsnorm(f, lp['g_post2'])
    return x, s_new


def setup_inputs(seed: int = 0) -> dict:
    key = jax.random.key(seed)
    ks = jax.random.split(key, 32)
    f32 = jnp.float32

    def nrm(i, shape, s):
        return jax.random.normal(ks[i], shape, f32) * s

    return {
        'x_prompt': nrm(0, (BATCH, SEQ, D_MODEL), 1.0),
        'x_sample': nrm(1, (DEC_BATCH, DEC_SEQ, D_MODEL), 1.0),
        'state_gla': nrm(2, (DEC_BATCH, DEPTH, 2, GLA_HEADS, GLA_DK, GLA_DV), 0.5),
        'c': nrm(3, (DEC_BATCH, D_MODEL), 1.0),
        'c_ctx': nrm(4, (D_MODEL,), 1.0),
        'w_mod': nrm(5, (DEPTH, D_MODEL, 6 * D_MODEL), D_MODEL ** -0.5),
        'b_mod': nrm(6, (DEPTH, 6 * D_MODEL), 0.02),
        'g_pre1': 1.0 + nrm(7, (DEPTH, D_MODEL), 0.05),
        'g_post1': 1.0 + nrm(8, (DEPTH, D_MODEL), 0.05),
        'g_pre2': 1.0 + nrm(9, (DEPTH, D_MODEL), 0.05),
        'g_post2': 1.0 + nrm(10, (DEPTH, D_MODEL), 0.05),
        'w_in': nrm(11, (DEPTH, D_MODEL, IN_COLS), D_MODEL ** -0.5),
        'conv_a': nrm(12, (DEPTH, 3, GROUP_W), 3 ** -0.5),
        'gla_wgk': nrm(13, (DEPTH, 2, GLA_RANK, GLA_HEADS * GLA_DK), GLA_RANK ** -0.5),
        'gla_bgk': nrm(14, (DEPTH, 2, GLA_HEADS * GLA_DK), 0.1),
        'gla_gnorm': 1.0 + nrm(15, (DEPTH, GLA_DV), 0.05),
        'mlp_ln': 1.0 + nrm(16, (DEPTH, GROUP_W), 0.05),
        'mlp_ws': nrm(17, (DEPTH, MLP_GROUPS, MLP_CHUNK, MLP_CHUNK), MLP_CHUNK ** -0.5),
        'mlp_bs': 1.0 + nrm(18, (DEPTH, MLP_GROUPS, MLP_CHUNK), 0.1),
        'hy_conv': nrm(19, (DEPTH, 3, D_COLS), 3 ** -0.5),
        'hy_w1': nrm(20, (DEPTH, HY_EMB, HY_HIDDEN), HY_EMB ** -0.5),
        'hy_b1': nrm(21, (DEPTH, HY_HIDDEN), 0.1),
        'hy_freq': 1.0 + nrm(22, (DEPTH, HY_HIDDEN), 0.1),
        'hy_w2': nrm(23, (DEPTH, HY_HIDDEN, HY_HIDDEN), HY_HIDDEN ** -0.5),
        'hy_b2': nrm(24, (DEPTH, HY_HIDDEN), 0.1),
        'hy_w3': nrm(25, (DEPTH, HY_HIDDEN, 2 * HY_ORDER * GROUP_W), 0.02),
        'hy_log_decay': jax.random.uniform(ks[26], (DEPTH, 2, HY_ORDER, GROUP_W), f32, math.log(3.0), math.log(15.0)),
        'hy_bias': nrm(27, (DEPTH, HY_ORDER, GROUP_W), 0.5),
        'w_out': nrm(28, (DEPTH, MIX_W, D_MODEL), MIX_W ** -0.5),
        'ffn_w1': nrm(29, (DEPTH, D_MODEL, D_FF), D_MODEL ** -0.5),
        'ffn_w3': nrm(30, (DEPTH, D_MODEL, D_FF), D_MODEL ** -0.5),
        'ffn_w2': nrm(31, (DEPTH, D_FF, D_MODEL), D_FF ** -0.5),
    }


def reference(x_prompt, x_sample, state_gla, c, c_ctx, w_mod, b_mod, g_pre1, g_post1, g_pre2, g_post2,
              w_in, conv_a, gla_wgk, gla_bgk, gla_gnorm, mlp_ln, mlp_ws, mlp_bs, hy_conv, hy_w1, hy_b1,
              hy_freq, hy_w2, hy_b2, hy_w3, hy_log_decay, hy_bias, w_out, ffn_w1, ffn_w3, ffn_w2):
    ctx_len = x_prompt.shape[1]
    s0_ctx = jnp.zeros((x_prompt.shape[0], 2, GLA_HEADS, GLA_DK, GLA_DV), jnp.float32)
    xc, xs = x_prompt, x_sample
    new_states = []
    for l in range(DEPTH):
        lp = dict(w_mod=w_mod[l], b_mod=b_mod[l], g_pre1=g_pre1[l], g_post1=g_post1[l],
                  g_pre2=g_pre2[l], g_post2=g_post2[l], w_in=w_in[l], conv_a=conv_a[l],
                  gla_wgk=gla_wgk[l], gla_bgk=gla_bgk[l], gla_gnorm=gla_gnorm[l],
                  mlp_ln=mlp_ln[l], mlp_ws=mlp_ws[l], mlp_bs=mlp_bs[l], hy_conv=hy_conv[l],
                  hy_w1=hy_w1[l], hy_b1=hy_b1[l], hy_freq=hy_freq[l], hy_w2=hy_w2[l], hy_b2=hy_b2[l],
                  hy_w3=hy_w3[l], hy_log_decay=hy_log_decay[l], hy_bias=hy_bias[l], w_out=w_out[l],
                  ffn_w1=ffn_w1[l], ffn_w3=ffn_w3[l], ffn_w2=ffn_w2[l])
        xc, s_ctx = trunk_layer(xc, c_ctx[None, :], s0_ctx, ctx_len, lp)
        new_states.append(s_ctx)
        xs, _ = trunk_layer(xs, c, state_gla[:, l], GRID_W, lp)
    new_state_gla = jnp.stack(new_states, axis=1)
    return (xc, xs, new_state_gla)
```

```python
import math
import numpy as np
from contextlib import ExitStack
import concourse.bass as bass
import concourse.mybir as mybir
from concourse.bass_utils import run_bass_kernel_spmd

F32 = mybir.dt.float32
BF16 = mybir.dt.bfloat16
AF = mybir.ActivationFunctionType
ALU = mybir.AluOpType
AX = mybir.AxisListType

ENGS = ['pe', 'act', 'dve', 'pool', 'sp']
NDSEM = 10
PI = math.pi


class Res:
    __slots__ = ('name', 'w', 'r', 'excl')

    def __init__(self, name):
        self.name = name
        self.w = None
        self.r = []
        self.excl = False


class Op:
    __slots__ = ('eng', 'fn', 'kind', 'deps', 'seq', 'inc', 'dsem', 'dval', 'prev', 'idx', 'tag')

    def __init__(self, eng, fn, kind, inc):
        self.eng = eng
        self.fn = fn
        self.kind = kind
        self.inc = inc
        self.deps = []
        self.seq = None
        self.dsem = None
        self.dval = None
        self.prev = None
        self.idx = 0


class Arena:
    def __init__(self):
        self.cur = []
        self.hist = {}
        self.hist_dma = []

    def next_phase(self):
        for r in self.cur:
            ops = list(r.r)
            if r.w is not None:
                ops.append(r.w)
            for o in ops:
                if o.kind == 'd':
                    self.hist_dma.append(o)
                else:
                    p = self.hist.get(o.eng)
                    if p is None or o.idx > p.idx:
                        self.hist[o.eng] = o
        self.cur = []
        self.hist_dma = self.hist_dma[-64:]

    def res(self, name="a"):
        r = Res(name)
        r.r = list(self.hist.values()) + list(self.hist_dma)
        self.cur.append(r)
        return r


class Prog:
    def __init__(self, nc):
        self.nc = nc
        self.es = ExitStack()
        self.ops = {e: [] for e in ENGS}
        self.ndma = {e: 0 for e in ENGS}
        self.esem = {e: self.es.enter_context(nc.semaphore("es_" + e)) for e in ENGS}
        self.dsem = {e: [self.es.enter_context(nc.semaphore("ds_%s%d" % (e, i))) for i in range(NDSEM)]
                     for e in ('sp', 'pool', 'act')}
        self.dlast = {e: [None] * NDSEM for e in ('sp', 'pool', 'act')}
        self.out_dmas = []
        self.nres = 0
        self.banks = []
        self.bank_i = 0
        self.flip = 0

    def sb(self, name, shape, dtype=F32):
        return self.es.enter_context(self.nc.sbuf_tensor(name, list(shape), dtype))

    def res(self, name=None):
        self.nres += 1
        return Res(name or ("r%d" % self.nres))

    def make_banks(self):
        for i in range(8):
            t = self.es.enter_context(self.nc.psum_tensor("pb%d" % i, [128, 512], F32))
            r = self.res("pb%d" % i)
            r.excl = True
            self.banks.append((t, r))

    def bank(self):
        nb = 7 if CFG.get("warm", 0) else 8
        b = self.banks[self.bank_i % nb]
        self.bank_i += 1
        return b

    def warm(self, n=None):
        n = CFG.get("warm", 0) if n is None else n
        if not n:
            return
        t, r = self.banks[7]
        lhs, rhs = self.warm_ops
        for _ in range(n):
            self.op('pe', lambda e: e.matmul(t[:, :], lhsT=lhs, rhs=rhs, start=True, stop=True), [], [], inc=False)

    def _add(self, o, rd, wr):
        deps = []
        for r in rd:
            if r.w is not None:
                deps.append((r.w, 'raw', r.name))
            if r.excl:
                for x in r.r:
                    if x.eng != o.eng:
                        deps.append((x, 'rar', r.name))
        for w in wr:
            if w.w is not None:
                deps.append((w.w, 'waw', w.name))
            for x in w.r:
                deps.append((x, 'war', w.name))
        for r in rd:
            r.r.append(o)
        for w in wr:
            w.w = o
            w.r = []
        o.deps = [(d, t, nm) for (d, t, nm) in deps if d is not o]
        o.idx = len(self.ops[o.eng])
        o.tag = getattr(self, 'tag', '')
        self.ops[o.eng].append(o)
        return o

    def op(self, eng, fn, rd=(), wr=(), inc=True):
        if eng == 'pool':
            eng = 'dve'
        if eng == 'gp':
            eng = 'pool' if CFG.get("use_pool", False) else 'dve'
        return self._add(Op(eng, fn, 'c', inc), rd, wr)

    def dma(self, eng, out, in_, rd=(), wr=(), is_out=False):
        o = Op(eng, lambda e: e.dma_start(out=out, in_=in_), 'd', True)
        n = self.ndma[eng]
        self.ndma[eng] = n + 1
        s = n % NDSEM
        o.dsem = self.dsem[eng][s]
        o.dval = 16 * (n // NDSEM + 1)
        o.prev = self.dlast[eng][s]
        self.dlast[eng][s] = o
        self._add(o, rd, wr)
        if is_out:
            self.out_dmas.append(o)
        return o

    def dma_fn(self, eng, fn, rd=(), wr=(), is_out=False):
        o = Op(eng, fn, 'd', True)
        n = self.ndma[eng]
        self.ndma[eng] = n + 1
        s = n % NDSEM
        o.dsem = self.dsem[eng][s]
        o.dval = 16 * (n // NDSEM + 1)
        o.prev = self.dlast[eng][s]
        self.dlast[eng][s] = o
        self._add(o, rd, wr)
        if is_out:
            self.out_dmas.append(o)
        return o

    def MM(self, out, lhsT, rhs, start, stop, rd, wr, inc=None):
        if inc is None:
            inc = stop
        return self.op('pe', lambda e: e.matmul(out, lhsT=lhsT, rhs=rhs, start=start, stop=stop), rd, wr, inc)

    def MMG(self, out, pairs, rd, wr):
        n = len(pairs)
        for i, (l, r) in enumerate(pairs):
            self.MM(out, l, r, i == 0, i == n - 1, rd, wr)

    def TR(self, out, in_, ident, rd, wr, inc=True):
        return self.op('pe', lambda e: e.transpose(out, in_, ident), rd, wr, inc)

    def ACT(self, out, in_, func, rd, wr, bias=None, scale=None):
        kw = {}
        if bias is not None:
            kw['bias'] = bias
        if scale is not None:
            kw['scale'] = scale
        return self.op('act', lambda e: e.activation(out=out, in_=in_, func=func, **kw), rd, wr)

    def CP(self, eng, out, in_, rd, wr):
        if eng == 'act':
            return self.op('act', lambda e: e.copy(out=out, in_=in_), rd, wr)
        return self.op(eng, lambda e: e.tensor_copy(out=out, in_=in_), rd, wr)

    def EV(self, out, in_, rd, wr):
        self.flip = (self.flip + 1) % 4
        return self.CP('act', out, in_, rd, wr)

    def TT(self, eng, out, in0, in1, op, rd, wr):
        return self.op(eng, lambda e: e.tensor_tensor(out=out, in0=in0, in1=in1, op=op), rd, wr)

    def TS(self, eng, out, in0, s1, s2, op0, op1, rd, wr):
        if s2 is None:
            return self.op(eng, lambda e: e.tensor_scalar(out=out, in0=in0, scalar1=s1, scalar2=None, op0=op0), rd, wr)
        return self.op(eng, lambda e: e.tensor_scalar(out=out, in0=in0, scalar1=s1, scalar2=s2, op0=op0, op1=op1), rd, wr)

    def STT(self, eng, out, in0, scalar, in1, op0, op1, rd, wr):
        eng = 'dve'
        return self.op(eng, lambda e: e.scalar_tensor_tensor(out=out, in0=in0, scalar=scalar, in1=in1, op0=op0, op1=op1), rd, wr)

    def MS(self, eng, out, val, rd, wr):
        return self.op(eng, lambda e: e.memset(out, val), rd, wr)

    def emit(self):
        nc = self.nc
        fin = Op('sp', None, 'c', False)
        fin.deps = [(d, 'raw', 'out') for d in self.out_dmas]
        self.ops['sp'].append(fin)
        for e in ENGS:
            cnt = 0
            pend = []
            lst = self.ops[e]
            last_c = None
            for o in lst:
                if o.kind == 'c' and o.fn is not None:
                    last_c = o
            if last_c is not None and e != 'sp':
                last_c.inc = True
            for o in lst:
                if o.kind != 'c' or o.fn is None:
                    continue
                if o.inc:
                    cnt += 1
                    o.seq = cnt
                    for p in pend:
                        p.seq = cnt
                    pend = []
                else:
                    pend.append(o)
            assert not pend or e == 'sp'
        self.stats = {e: len(self.ops[e]) for e in ENGS}
        self.selfwaits = {}

        def run(e, eng):
            waited = {}
            esem = self.esem

            def wait(sem, val):
                k = id(sem)
                if waited.get(k, 0) >= val:
                    return False
                waited[k] = val
                eng.wait_ge(sem, val)
                return True

            for o in self.ops[e]:
                for (d, t, nm) in o.deps:
                    if d.kind == 'd':
                        wait(d.dsem, d.dval)
                    else:
                        if d.eng == e:
                            if e == 'pe':
                                continue
                            if o.kind == 'c' and t != 'raw' and not CFG.get("strict_same_engine", True):
                                continue
                        if wait(esem[d.eng], d.seq) and d.eng == e:
                            key = (e, t, nm)
                            self.selfwaits[key] = self.selfwaits.get(key, 0) + 1
                if o.fn is None:
                    continue
                if o.kind == 'd':
                    if o.prev is not None:
                        wait(o.dsem, o.prev.dval)
                    o.fn(eng).then_inc(o.dsem, 16)
                else:
                    ins = o.fn(eng)
                    if o.inc:
                        ins.then_inc(esem[e], 1)

        with nc.Block() as block:
            @block.tensor
            def _(eng):
                run('pe', eng)

            @block.scalar
            def _(eng):
                run('act', eng)

            @block.vector
            def _(eng):
                run('dve', eng)

            @block.gpsimd
            def _(eng):
                run('pool', eng)

            @block.sync
            def _(eng):
                run('sp', eng)
        self.es.close()


D = 1024
DEPTH = 2
NTOK = 1024
IN_COLS = 2848
D_FF = 2816
EPS = 1e-6
CFG = {"mixers": "ABCD", "groups": "PS", "layers": 2, "strict_same_engine": False}


_WARM = [None]


def zip_run(gens):
    gens = list(gens)
    while gens:
        for g in list(gens):
            try:
                next(g)
            except StopIteration:
                gens.remove(g)
        if _WARM[0] is not None:
            _WARM[0]()


def host_consts():
    c = {}
    c["ident"] = np.eye(128, dtype=np.float32)
    j = np.arange(128)[:, None]
    i = np.arange(128)[None, :]
    tri = np.zeros((128, 4, 128), np.float32)
    tri[:, 0, :] = (j <= i) * (-1.0 / 16)
    tri[:, 1, :] = (j > i) * (-1.0 / 16)
    tri[:, 2, :] = (j >= i) * (-1.0 / 16)
    tri[:, 3, :] = (j < i) * (-1.0 / 16)
    c["tri"] = tri
    cm = np.zeros((128, 2, 128), np.float32)
    cm[:, 0, :] = (j <= i)
    cm[:, 1, :] = (j >= i)
    c["cmask"] = cm
    hm = np.zeros((128, 4, 128), np.float32)
    for h in range(4):
        hm[32 * h:32 * h + 32, h, :] = 1.0
    c["hm4"] = hm
    bd = np.zeros((128, 128), np.float32)
    bd[:64, :64] = 1.0 / 64
    bd[64:, 64:] = 1.0 / 64
    c["bd64"] = bd
    c["onesd"] = np.full((128, 128), 1.0 / 1024, np.float32)
    alt = np.where(np.arange(1024) % 2 == 0, 1.0, -1.0).astype(np.float32)
    c["altr"] = alt[None, :].copy()
    c["altc"] = alt[:128, None].copy()
    c["onesr"] = np.ones((1, 128), np.float32)
    for L in (256, 1024):
        n = 2 * L
        t = np.arange(L, dtype=np.float64)
        th = 2.0 * np.pi * np.outer(t, t) / n
        c["cs%d" % L] = np.concatenate([np.cos(th), np.sin(th)], axis=1).astype(np.float32)
        w = np.full((128, L // 128), 2.0 / n, np.float32)
        w[0, 0] = 1.0 / n
        c["wcol%d" % L] = w
        pos = np.arange(L, dtype=np.float32)
        tt = pos / np.float32(max(L - 1, 1))
        omega = (np.float32(2.0 * math.pi) * pos / np.float32(L)).astype(np.float32)
        bands = np.linspace(1e-4, 7, 8, dtype=np.float32)
        feat = np.concatenate([tt[:, None], np.cos(omega[:, None] * bands), np.sin(omega[:, None] * bands)], axis=-1)
        c["feat%d" % L] = np.ascontiguousarray(feat.T.astype(np.float32))
        c["negt%d" % L] = np.ascontiguousarray((-tt).reshape(L // 128, 128).T.astype(np.float32))
    return c


CONST_SHAPES = {k: v.shape for k, v in host_consts().items()}

WEIGHT_SHAPES = {
    'w_mod': (2, 1024, 6144), 'b_mod': (2, 6144), 'g_pre1': (2, 1024), 'g_post1': (2, 1024),
    'g_pre2': (2, 1024), 'g_post2': (2, 1024), 'w_in': (2, 1024, 2848), 'conv_a': (2, 3, 256),
    'gla_wgk': (2, 2, 16, 128), 'gla_bgk': (2, 2, 128), 'gla_gnorm': (2, 64), 'mlp_ln': (2, 256),
    'mlp_ws': (2, 4, 128, 128), 'mlp_bs': (2, 4, 128), 'hy_conv': (2, 3, 768), 'hy_w1': (2, 17, 64),
    'hy_b1': (2, 64), 'hy_freq': (2, 64), 'hy_w2': (2, 64, 64), 'hy_b2': (2, 64), 'hy_w3': (2, 64, 1024),
    'hy_log_decay': (2, 2, 2, 256), 'hy_bias': (2, 2, 256), 'w_out': (2, 1024, 1024),
    'ffn_w1': (2, 1024, 2816), 'ffn_w3': (2, 1024, 2816), 'ffn_w2': (2, 2816, 1024),
}


def build():
    nc = bass.Bass("TRN2", target_bir_lowering=False)

    def din(name, shape):
        return nc.dram_tensor(name, list(shape), F32, kind="ExternalInput").ap()

    def dout(name, shape):
        return nc.dram_tensor(name, list(shape), F32, kind="ExternalOutput").ap()

    xp_d = din("xp", (NTOK, D))
    xs_d = din("xs", (NTOK, D))
    s0_d = din("s0", (2, 2, 4, 32, 64))
    cv_d = din("cv", (2, D))
    W = {k: din(k, s) for k, s in WEIGHT_SHAPES.items()}
    C = {k: din("c_" + k, s) for k, s in CONST_SHAPES.items()}
    yp_d = dout("yp", (NTOK, D))
    ys_d = dout("ys", (NTOK, D)) if not CFG.get("own", True) else None
    ysq_d = dout("ysq", (256, D))
    qoff_d = nc.dram_tensor("qoff", [1, 1], mybir.dt.int32, kind="ExternalInput").ap()
    dyn = {}
    ns_d = dout("ns", (4, 2, 2, 128, 64))

    P = Prog(nc)
    P.make_banks()
    sb = P.sb

    def _setup_q(e):
        reg = e.alloc_register("qoff")
        e.reg_load(reg, qoff_d[0:1, 0:1])
        dyn['c'] = e.snap(reg, min_val=0, max_val=768)
        return None

    P.op('sp', _setup_q, inc=False)

    XT = sb("XT", [128, 8, NTOK], F32)
    rXT = [P.res("XT0"), P.res("XT1")]
    HT = sb("HT", [128, 8, NTOK], BF16)
    rHT = [[P.res("HT%d_%d" % (b_, f_)) for f_ in range(8)] for b_ in range(2)]
    BIG = sb("BIG", [128, 11264], F32)
    aBIG = Arena()
    AR2 = sb("AR2", [128, 8192], F32)
    aAR2 = Arena()
    WBR = sb("WBR", [128, 16384], BF16)
    rWB = [P.res("WB0"), P.res("WB1")]
    SQ = sb("SQ", [128, 8, 512], BF16)
    rSQ = P.res("SQ")
    rSQc = [P.res("SQ%d" % i) for i in range(4)]
    RS = sb("RS", [128, 512], F32)
    rRS = P.res("RS")
    TMPF = sb("TMPF", [128, 2, 512], F32)
    rTMPF = [P.res("TMPF0"), P.res("TMPF1")]
    TTB = sb("TTB", [128, 4096], F32)
    aTT = Arena()
    XS = TTB[:, 0:2048].rearrange("p (k d) -> p k d", k=2)
    CS256 = sb("CS256", [128, 2, 512], BF16)
    rCS256 = P.res()
    ident = sb("ident", [128, 128], F32)
    tri = sb("tri", [128, 4, 128], F32)
    cmask = sb("cmask", [128, 2, 128], F32)
    hm4 = sb("hm4", [128, 4, 128], F32)
    bd64 = sb("bd64", [128, 128], BF16)
    onesd = sb("onesd", [128, 128], BF16)
    altr = sb("altr", [1, 1024], BF16)
    altc = sb("altc", [128, 1], BF16)
    onesr = sb("onesr", [1, 128], BF16)
    wcol256 = sb("wcol256", [128, 2], F32)
    wcol1024 = sb("wcol1024", [128, 8], F32)
    negt256 = sb("negt256", [128, 2], F32)
    negt1024 = sb("negt1024", [128, 8], F32)
    rC = P.res("consts")
    cst = sb("cst", [128, 8], F32)
    rcst = P.res("cst")
    PST = AR2[:, 0:384].rearrange("p (a b) -> p a b", a=3)
    rPST = aAR2.res("PST")
    PAR = sb("PAR", [128, 3, 128], F32)
    rPAR = P.res("PAR")
    MOD = sb("MOD", [128, 2, 48, 2], F32)
    rMOD = P.res("MOD")
    GG = sb("GG", [128, 2, 4, 8, 2], F32)
    rGG = P.res("GG")
    SCB = sb("SCB", [128, 8, 2], BF16)
    rSCB = P.res("SCB")
    WG = sb("WG", [32, 2, 2, 128], BF16)
    BGK = sb("BGK", [1, 2, 2, 128], BF16)
    HW1 = sb("HW1", [17, 2, 64], F32)
    HW2 = sb("HW2", [64, 2, 64], F32)
    HFB = sb("HFB", [64, 2, 2], F32)
    rSW = P.res("smallw")
    SST = sb("SST", [128, 4, 256], F32)
    rSST = [P.res("S%d" % i) for i in range(4)]
    SBF = sb("SBF", [128, 2, 256], BF16)
    rSBF = [P.res("SBF0"), P.res("SBF1")]
    SCMP = sb("SCMP", [128, 2, 64], F32)
    rSCMP = [P.res("SC0"), P.res("SC1")]

    P.warm_ops = (onesd[:], CS256[:, 0, :])
    _WARM[0] = P.warm
    WB = [WBR[:, 0:8192], WBR[:, 8192:16384]]
    wbi = [0]

    mods_pending = [None]

    def wbuf():
        if mods_pending[0] is not None:
            if mods_pending[0] >= 1:
                next(_mgh[0])
            else:
                mods_pending[0] += 1
        k = wbi[0] % 2
        wbi[0] += 1
        return WB[k], rWB[k]

    for (t, k) in ((ident, "ident"), (tri, "tri"), (cmask, "cmask"), (hm4, "hm4"), (wcol256, "wcol256"),
                   (wcol1024, "wcol1024"), (negt256, "negt256"), (negt1024, "negt1024")):
        P.dma('sp', t[:], C[k], wr=[rC])
    for (t, k) in ((bd64, "bd64"), (onesd, "onesd"), (altr, "altr"), (altc, "altc"), (onesr, "onesr")):
        P.dma('pool', t[:], C[k], wr=[rC])
    P.dma('pool', CS256[:], C["cs256"].rearrange("(tc p) n -> p tc n", p=128), wr=[rCS256])
    P.MS('dve', cst[:, 0:1], EPS, [], [rcst])
    P.MS('dve', cst[:, 1:2], -PI, [], [rcst])
    P.MS('dve', cst[:, 2:3], 1.0, [], [rcst])
    P.MS('dve', cst[:, 3:4], 0.0, [], [rcst])
    epsc = cst[:, 0:1]

    P.MS('pool', PST[:], 0.0, [], [rPST])
    for l in range(2):
        P.dma('sp', PST[l * 48:(l + 1) * 48, 0, :], W['b_mod'][l].rearrange("(c p) -> c p", p=128), wr=[rPST])
    for l in range(2):
        for wi, nm in enumerate(('g_pre1', 'g_post1', 'g_pre2', 'g_post2')):
            r0 = l * 32 + wi * 8
            P.dma('sp', PST[r0:r0 + 8, 1, :], W[nm][l].rearrange("(c p) -> c p", p=128), wr=[rPST])
    for v in range(2):
        P.dma('sp', PST[64 + v * 8:72 + v * 8, 1, :], cv_d[v].rearrange("(c p) -> c p", p=128), wr=[rPST])
    for l in range(2):
        P.dma('sp', PST[80 + l * 6:86 + l * 6, 1, :], W['conv_a'][l].rearrange("k (c p) -> (k c) p", p=128), wr=[rPST])
        P.dma('sp', PST[92 + l:93 + l, 1, 0:64], W['gla_gnorm'][l:l + 1, :], wr=[rPST])
        P.dma('sp', PST[92 + l:93 + l, 1, 64:128], W['gla_gnorm'][l:l + 1, :], wr=[rPST])
        P.dma('sp', PST[94 + l:95 + l, 1, 0:64], W['hy_freq'][l:l + 1, :], wr=[rPST])
        P.dma('sp', PST[96 + l:97 + l, 1, 0:64], W['hy_b1'][l:l + 1, :], wr=[rPST])
        P.dma('sp', PST[98 + l:99 + l, 1, 0:64], W['hy_b2'][l:l + 1, :], wr=[rPST])
        P.dma('sp', PST[l * 18:(l + 1) * 18, 2, :], W['hy_conv'][l].rearrange("k (c p) -> (k c) p", p=128), wr=[rPST])
    for i in range(3):
        b, rb = P.bank()
        P.TR(b[:, 0:128], PST[:, i, :], ident[:], [rPST, rC], [rb])
        P.EV(PAR[:, i, :], b[:, 0:128], [rb], [rPAR])

    def bmod_col(l, c0, c1):
        return PAR[:, 0, l * 48 + c0:l * 48 + c1]

    def gv_col(l, wi):
        return PAR[:, 1, l * 32 + wi * 8:l * 32 + wi * 8 + 8]

    def conva_col(l, k, ch):
        c = 80 + l * 6 + k * 2 + ch
        return PAR[:, 1, c:c + 1]

    def hyconv_col(l, k, ci):
        c = l * 18 + k * 6 + ci
        return PAR[:, 2, c:c + 1]

    def gn_col(l):
        return PAR[:, 1, 92 + l:93 + l]

    def freq_col(l):
        return PAR[0:64, 1, 94 + l:95 + l]

    P.MS('pool', WG[:], 0.0, [], [rSW])
    for l in range(2):
        for d in range(2):
            P.dma('pool', WG[d * 16:(d + 1) * 16, l, d, :], W['gla_wgk'][l, d], wr=[rSW])
        P.dma('pool', BGK[0:1, l, :, :], W['gla_bgk'][l:l + 1], wr=[rSW])
        P.dma('sp', HW1[:, l, :], W['hy_w1'][l], wr=[rSW])
        P.dma('sp', HW2[:, l, :], W['hy_w2'][l], wr=[rSW])
        P.TT('dve', HFB[:, l, 0:1], PAR[0:64, 1, 94 + l:95 + l], PAR[0:64, 1, 96 + l:97 + l], ALU.mult, [rPAR], [rSW])
        P.TT('dve', HFB[:, l, 1:2], PAR[0:64, 1, 94 + l:95 + l], PAR[0:64, 1, 98 + l:99 + l], ALU.mult, [rPAR], [rSW])

    P.ACT(SCB[:].rearrange("p k v -> p v k"), PAR[:, 1, 64:80].rearrange("p (v k) -> p v k", v=2), AF.Silu, [rPAR], [rSCB])
    rMODw = [[P.res("MOD%d_%d" % (l, w)) for w in range(6)] for l in range(2)]
    rGGw = [[P.res("GG%d_%d" % (l, w)) for w in range(4)] for l in range(2)]
    _mgh = [None]

    def mods_gen():
      for l in range(CFG["layers"]):
        for blk in range(6):
            wb, rw = wbuf()
            wv = wb[:, 0:8192].rearrange("p (k n) -> p k n", k=8)
            P.dma('pool', wv, W['w_mod'][l][:, blk * 1024:(blk + 1) * 1024].rearrange("(k p) n -> p k n", p=128), wr=[rw])
            mods_pending[0] = 0
            yield
            mods_pending[0] = None
            b, rb = P.bank()
            for j in range(8):
                P.MMG(b[:, 2 * j:2 * j + 2], [(wv[:, kc, j * 128:(j + 1) * 128], SCB[:, kc, :]) for kc in range(8)],
                      [rw, rSCB], [rb])
            P.TT('dve', MOD[:, l, blk * 8:(blk + 1) * 8, :], b[:, 0:16].rearrange("p (j v) -> p j v", v=2),
                 bmod_col(l, blk * 8, blk * 8 + 8).unsqueeze(2).to_broadcast([128, 8, 2]), ALU.add, [rb, rPAR], [rMODw[l][blk]])
            if blk == 1:
                P.STT('dve', GG[:, l, 0], MOD[:, l, 8:16, :], 1.0, gv_col(l, 0).unsqueeze(2).to_broadcast([128, 8, 2]),
                      ALU.add, ALU.mult, [rMODw[l][1], rPAR], [rGGw[l][0]])
            if blk == 4:
                P.STT('dve', GG[:, l, 1], MOD[:, l, 32:40, :], 1.0, gv_col(l, 2).unsqueeze(2).to_broadcast([128, 8, 2]),
                      ALU.add, ALU.mult, [rMODw[l][4], rPAR], [rGGw[l][1]])
            if blk == 2:
                P.TT('dve', GG[:, l, 2], MOD[:, l, 16:24, :], gv_col(l, 1).unsqueeze(2).to_broadcast([128, 8, 2]),
                     ALU.mult, [rMODw[l][2], rPAR], [rGGw[l][2]])
            if blk == 5:
                P.TT('dve', GG[:, l, 3], MOD[:, l, 40:48, :], gv_col(l, 3).unsqueeze(2).to_broadcast([128, 8, 2]),
                     ALU.mult, [rMODw[l][5], rPAR], [rGGw[l][3]])
            mods_done[0] += 1
            yield

    mods_done = [0]
    _mg = mods_gen()
    _mgh[0] = _mg

    def mods_until(l, blk):
        while mods_done[0] < l * 6 + blk + 1:
            try:
                next(_mg)
            except StopIteration:
                return

    def mods_step(n=1):
        for _ in range(n):
            try:
                next(_mg)
            except StopIteration:
                return

    mods_want = [False]

    def mods_dma():
        mods_want[0] = True

    def after_load():
        if mods_want[0]:
            mods_want[0] = False
            if mods_pending[0] is None:
                mods_step(1)

    def mods_mm():
        if mods_pending[0] is not None:
            mods_step(1)


    def rstd_of(src, nch, rd, lhs_ones, n=512):
        b, rb = P.bank()
        if nch == 8:
            for i in range(4):
                P.ACT(SQ[:, 2 * i:2 * i + 2, 0:n], src[:, 2 * i:2 * i + 2, :], AF.Square, rd, [rSQc[i]])
                for fc in (2 * i, 2 * i + 1):
                    P.MM(b[:, 0:n], lhs_ones, SQ[:, fc, 0:n], fc == 0, fc == 7, [rSQc[i], rC], [rb])
        else:
            P.ACT(SQ[:, 0:nch, 0:n], src, AF.Square, rd, rSQc)
            P.MMG(b[:, 0:n], [(lhs_ones, SQ[:, fc, 0:n]) for fc in range(nch)], rSQc + [rC], [rb])
        P.ACT(RS[:, 0:n], b[:, 0:n], AF.Ln, [rb, rcst], [rRS], bias=epsc, scale=1.0)
        P.ACT(RS[:, 0:n], RS[:, 0:n], AF.Exp, [rRS], [rRS], scale=-0.5)

    def norm_mod(l, blk, gi, sh0, v, sl=None):
        mods_until(l, 1 if gi == 0 else 4)
        if sl is None:
            sl = slice(blk * 512, (blk + 1) * 512)
        n = sl.stop - sl.start
        rstd_of(XT[:, :, sl], 8, [rXT[blk]], onesd[:], n)
        for fc in range(8):
            P.STT('dve', TMPF[:, fc % 2, 0:n], XT[:, fc, sl], GG[:, l, gi, fc, v:v + 1], RS[:, 0:n], ALU.mult, ALU.mult,
                  [rXT[blk], rGGw[l][gi], rRS], [rTMPF[fc % 2]])
            P.ACT(HT[:, fc, sl], TMPF[:, fc % 2, 0:n], AF.Identity, [rTMPF[fc % 2], rMODw[l][sh0 // 8]], [rHT[blk][fc]],
                  bias=MOD[:, l, sh0 + fc, v:v + 1], scale=1.0)

    def fproj(wv, rw, c0, m, blk, sl=None):
        b, rb = P.bank()
        if sl is None:
            sl = slice(blk * 512, (blk + 1) * 512)
        n = sl.stop - sl.start
        P.MMG(b[0:m, 0:n], [(wv[:, kc, c0:c0 + m], HT[:, kc, sl]) for kc in range(8)], [rw] + rHT[blk], [rb])
        return b, rb

    def tproj(wv, rw, c0, n, tt):
        b, rb = P.bank()
        P.MMG(b[:, 0:n], [(HT[:, kc, tt * 128:(tt + 1) * 128], wv[:, kc, c0:c0 + n]) for kc in range(8)],
              [rw] + rHT[tt // 4], [rb])
        return b, rb

    def load_w_in(l, c0, n):
        wb, rw = wbuf()
        wv = wb[:, 0:8 * n].rearrange("p (k n) -> p k n", k=8)
        P.dma('pool', wv, W['w_in'][l][:, c0:c0 + n].rearrange("(k p) n -> p k n", p=128), wr=[rw])
        after_load()
        return wv, rw

    def conv3(dst, src, w0, w1, w2, row_w, rd, wr):
        P.ACT(dst, src, AF.Identity, rd, wr, scale=w1)
        d3 = dst.rearrange("p (r c) -> p r c", c=row_w)
        s3 = src.rearrange("p (r c) -> p r c", c=row_w)
        P.STT('dve', d3[:, :, 1:row_w], s3[:, :, 0:row_w - 1], w0, d3[:, :, 1:row_w], ALU.mult, ALU.add, rd + wr, wr)
        P.STT('pool', d3[:, :, 0:row_w - 1], s3[:, :, 1:row_w], w2, d3[:, :, 0:row_w - 1], ALU.mult, ALU.add, rd + wr, wr)

    def run_group(gname):
        is_p = gname == "P"
        x_d = xp_d if is_p else xs_d
        y_d = yp_d if is_p else ys_d
        v = 0 if is_p else 1
        row_w = 256 if is_p else 64
        L = 256 if is_p else 1024
        ntl = L // 128
        seqs = [(s * 2, 2) for s in range(4)] if is_p else [(0, 8)]

        P.tag = "LOAD" + gname + (str(l) if "l" in dir() else "")
        aTT.next_phase()
        rXS = [aTT.res("XS0"), aTT.res("XS1")]
        for tt in range(8):
            k = tt % 2
            P.dma('sp', XS[:, k, :], x_d[tt * 128:(tt + 1) * 128, :], wr=[rXS[k]])
            for half in range(2):
                b, rb = P.bank()
                for j in range(4):
                    P.TR(b[:, j * 128:(j + 1) * 128], XS[:, k, (half * 4 + j) * 128:(half * 4 + j + 1) * 128], ident[:],
                         [rXS[k], rC], [rb], inc=(j == 3))
                P.EV(XT[:, half * 4:half * 4 + 4, tt * 128:(tt + 1) * 128], b[:, :].rearrange("p (j t) -> p j t", j=4),
                     [rb], [rXT[tt // 4]])
        mods_step(4)

        for l in range(CFG["layers"]):
            if l > 0:
                pass
            P.tag = "N1" + gname + (str(l) if "l" in dir() else "")
            for blk in range(2):
                norm_mod(l, blk, 0, 0, v)

            mods_dma()
            aBIG.next_phase()
            MIN = BIG[:, 0:4096].bitcast(BF16).rearrange("p (c t) -> p c t", c=8)
            rMIN = [aBIG.res("MIN%d" % i) for i in range(8)]
            PTMP = BIG[:, 4096:10240].rearrange("p (c t) -> p c t", c=6)
            rPT = [aBIG.res("PT%d" % i) for i in range(6)]
            for ci in range(8):
                ch = "AABBCCDD"[ci]
                if ch not in CFG["mixers"]:
                    P.MS('pool', MIN[:, ci, :], 0.0, [], [rMIN[ci]])

            P.tag = "A" + gname + (str(l) if "l" in dir() else "")
            if "A" in CFG["mixers"]:
                aAR2.next_phase()
                CG = AR2[:, 0:1024]
                U = AR2[:, 1024:2048]
                ACC = AR2[:, 2048:3072]
                BG = AR2[:, 3072:4096]
                rCG, rU, rACC, rBG = (aAR2.res() for _ in range(4))
                wv, rw = load_w_in(l, 0, 768)
                for ch in range(2):
                    for blk in range(2):
                        sl = slice(blk * 512, (blk + 1) * 512)
                        b, rb = fproj(wv, rw, 256 + ch * 128, 128, blk)
                        P.CP('act', CG[:, sl], b[:, :], [rb], [rCG])
                        b2, rb2 = fproj(wv, rw, 512 + ch * 128, 128, blk)
                        P.TT('dve', U[:, sl], CG[:, sl], b2[:, :], ALU.mult, [rCG, rb2], [rU])
                        b3, rb3 = fproj(wv, rw, ch * 128, 128, blk)
                        P.CP('act', BG[:, sl], b3[:, :], [rb3], [rBG])
                    mods_mm()
                    conv3(ACC, U, conva_col(l, 0, ch), conva_col(l, 1, ch), conva_col(l, 2, ch), row_w, [rU, rPAR], [rACC])
                    P.TT('dve', MIN[:, ch, :], ACC, BG, ALU.mult, [rACC, rBG], [rMIN[ch]])

            mods_mm()
            mods_dma()
            P.tag = "B" + gname + (str(l) if "l" in dir() else "")
            if "B" in CFG["mixers"]:
                aAR2.next_phase()
                QT = AR2[:, 0:1024]
                KT = AR2[:, 1024:2048]
                GKT = AR2[:, 2048:2560].bitcast(BF16)
                K_T = AR2[:, 2560:3584].rearrange("p (t k) -> p t k", t=8)
                V_T = AR2[:, 3584:4608].bitcast(BF16).rearrange("p (t k) -> p t k", t=8)
                OT = AR2[:, 4608:6656].rearrange("p (h t) -> p h t", h=2)
                SG = AR2[:, 6656:7680].bitcast(BF16).rearrange("p (h t) -> p h t", h=2)
                rQT, rKT, rGKT, rK_T, rV_T, rSG = (aAR2.res() for _ in range(6))
                rOT = [[aAR2.res() for _ in range(8)] for _ in range(2)]
                wv, rw = load_w_in(l, 768, 800)
                aTT.next_phase()
                gslots = [[aTT.res() for _ in range(6)] + [aTT.res(), aTT.res(), aTT.res(), rTMPF[sl_]] * 1 + [aTT.res(), aTT.res(), aTT.res(), rTMPF[sl_]] for sl_ in range(2)]
                for blk in range(2):
                    sl = slice(blk * 512, (blk + 1) * 512)
                    b, rb = fproj(wv, rw, 0, 128, blk)
                    P.CP('act', QT[:, sl], b[:, :], [rb], [rQT])
                    b, rb = fproj(wv, rw, 128, 128, blk)
                    P.CP('dve', KT[:, sl], b[:, :], [rb], [rKT])
                    b, rb = fproj(wv, rw, 768, 32, blk)
                    P.CP('act', GKT[0:32, sl], b[0:32, :], [rb], [rGKT])
                for tt in range(8):
                    b, rb = tproj(wv, rw, 128, 384, tt)
                    P.CP('act', K_T[:, tt, :], b[:, 0:128], [rb], [rK_T])
                    P.CP('dve', V_T[:, tt, :], b[:, 128:384], [rb], [rV_T])
                for hp in range(2):
                    P.MS('dve', OT[:, hp, :], 0.0, [], rOT[hp])
                P.tag = "Bs" + gname + str(l)

                def gla_chain(d, si, t0, nt, slot):
                    sidx = d * 2 + (si % 2)
                    S = SST[:, sidx, :]
                    rS = rSST[sidx]
                    P.MS('dve', S, 0.0, [], [rS])
                    if not is_p:
                        for h in range(4):
                            P.dma('sp', SST[32 * h:32 * h + 32, sidx, 64 * h:64 * h + 64], s0_d[l, d, h], rd=[], wr=[rS])
                    P.CP('act', SBF[:, d, :], S, [rS], [rSBF[d]])
                    yield
                    order = list(range(t0, t0 + nt)) if d == 0 else list(range(t0 + nt - 1, t0 - 1, -1))
                    o_ = slot * 1800
                    LP = TTB[:, o_ + 0:o_ + 128]
                    EBD = TTB[:, o_ + 128:o_ + 384]
                    EB = EBD[:, 0:128]
                    ED = EBD[:, 128:256]
                    ENB = TTB[:, o_ + 384:o_ + 512]
                    QE = TTB[:, o_ + 512:o_ + 640]
                    KTL = TTB[:, o_ + 640:o_ + 704].bitcast(BF16)
                    KH = TTB[:, o_ + 704:o_ + 768].bitcast(BF16)
                    rLP, rEBD, rENB, rQE, rKTL, rKH = gslots[slot][0:6]
                    Q4s, AMs, USs, ACs, rQ4s, rAMs, rUSs, rACs = [], [], [], [], [], [], [], []
                    for ps in range(2):
                        q_ = o_ + 768 + ps * 516
                        Q4s.append(TTB[:, q_:q_ + 256].bitcast(BF16))
                        AMs.append(TTB[:, q_ + 256:q_ + 512].bitcast(BF16))
                        ACs.append(TTB[:, q_ + 512:q_ + 513])
                        USs.append(TMPF[:, slot, ps * 256:(ps + 1) * 256])
                        rQ4s.append(gslots[slot][6 + ps * 4])
                        rAMs.append(gslots[slot][7 + ps * 4])
                        rACs.append(gslots[slot][8 + ps * 4])
                        rUSs.append(gslots[slot][9 + ps * 4])

                    def prep(tt, ps):
                        tsl = slice(tt * 128, (tt + 1) * 128)
                        Q4, AM, US, AC = Q4s[ps], AMs[ps], USs[ps], ACs[ps]
                        rQ4, rAM, rUS, rAC = rQ4s[ps], rAMs[ps], rUSs[ps], rACs[ps]
                        b, rb = P.bank()
                        P.MM(b[:, 0:128], GKT[0:32, tsl], WG[0:32, l, d, :], True, False, [rGKT, rSW], [rb])
                        P.MM(b[:, 0:128], onesr[0:1, :], BGK[0:1, l, d, :], False, True, [rC, rSW], [rb])
                        yield
                        P.ACT(LP, b[:, 0:128], AF.Exp, [rb], [rLP], scale=-1.0)
                        P.ACT(LP, LP, AF.Ln, [rLP], [rLP], bias=1.0, scale=1.0)
                        yield
                        b1, rb1 = P.bank()
                        P.MM(b1[:, 0:128], LP, tri[:, d * 2, :], True, True, [rLP, rC], [rb1])
                        P.MM(b1[:, 128:256], tri[:, d * 2 + 1, :], LP, True, True, [rLP, rC], [rb1])
                        yield
                        P.ACT(EBD, b1[:, 0:256], AF.Exp, [rb1], [rEBD])
                        P.ACT(ENB, b1[:, 0:128], AF.Exp, [rb1], [rENB], scale=-1.0)
                        yield
                        P.STT('dve', QE, QT[:, tsl], 32.0 ** -0.5, EB, ALU.mult, ALU.mult, [rQT, rEBD], [rQE])
                        P.TT('gp', Q4.rearrange("p (h i) -> p h i", h=4), QE.unsqueeze(1).to_broadcast([128, 4, 128]),
                             hm4[:], ALU.mult, [rQE, rC], [rQ4])
                        P.TT('dve', KTL, KT[:, tsl], ENB, ALU.mult, [rKT, rENB], [rKTL])
                        P.TT('gp', KH, K_T[:, tt, :], ED, ALU.mult, [rK_T, rEBD], [rKH])
                        P.CP('act', AC, EB[:, 127:128] if d == 0 else EB[:, 0:1], [rEBD], [rAC])
                        yield
                        b2, rb2 = P.bank()
                        P.MM(b2[:, :], KTL, Q4, True, True, [rKTL, rQ4], [rb2])
                        b3, rb3 = P.bank()
                        P.MM(b3[:, 0:256], KH, V_T[:, tt, :], True, True, [rKH, rV_T], [rb3])
                        yield
                        P.TT('dve', AM.rearrange("p (h i) -> p h i", h=4), b2[:, :].rearrange("p (h i) -> p h i", h=4),
                             cmask[:, d, :].unsqueeze(1).to_broadcast([128, 4, 128]), ALU.mult, [rb2, rC], [rAM])
                        P.CP('act', US, b3[:, 0:256], [rb3], [rUS])
                        yield

                    def rec(tt, ps):
                        tsl = slice(tt * 128, (tt + 1) * 128)
                        Q4, AM, US, AC = Q4s[ps], AMs[ps], USs[ps], ACs[ps]
                        rQ4, rAM, rUS, rAC = rQ4s[ps], rAMs[ps], rUSs[ps], rACs[ps]
                        b4, rb4 = P.bank()
                        for hp in range(2):
                            P.MM(b4[:, hp * 256:(hp + 1) * 256], V_T[:, tt, hp * 128:(hp + 1) * 128],
                                 AM[:, hp * 256:(hp + 1) * 256], True, False, [rV_T, rAM], [rb4])
                            P.MM(b4[:, hp * 256:(hp + 1) * 256], SBF[:, d, hp * 128:(hp + 1) * 128],
                                 Q4[:, hp * 256:(hp + 1) * 256], False, True, [rSBF[d], rQ4], [rb4])
                        yield
                        P.STT('dve', S, S, AC, US, ALU.mult, ALU.add, [rS, rAC, rUS], [rS])
                        P.CP('act', SBF[:, d, :], S, [rS], [rSBF[d]])
                        yield
                        for hl in range(2):
                            rows = slice(hl * 64, (hl + 1) * 64)
                            src = b4[rows, hl * 128:hl * 128 + 384].rearrange("p (h c) -> p h c", c=128)[:, 0:3:2, :]
                            P.TT('dve', OT[rows, :, tsl], OT[rows, :, tsl], src, ALU.add,
                                 [rb4, rOT[0][tt], rOT[1][tt]], [rOT[0][tt], rOT[1][tt]])
                            yield

                    yield from prep(order[0], 0)
                    for k in range(nt):
                        subs = [rec(order[k], k % 2)]
                        if k + 1 < nt:
                            subs.append(prep(order[k + 1], (k + 1) % 2))
                        while subs:
                            for g in list(subs):
                                try:
                                    next(g)
                                except StopIteration:
                                    subs.remove(g)
                            yield
                    if is_p:
                        k = (d * 4 + si) % 2
                        for h in range(4):
                            P.CP('act', SCMP[32 * h:32 * h + 32, k, :], SST[32 * h:32 * h + 32, sidx, 64 * h:64 * h + 64],
                                 [rS], [rSCMP[k]])
                        P.dma('sp', ns_d[si, l, d], SCMP[:, k, :], rd=[rSCMP[k]], is_out=True)
                    yield

                for si, (t0, nt) in enumerate(seqs):
                    mods_mm()
                    zip_run([gla_chain(0, si, t0, nt, 0), gla_chain(1, si, t0, nt, 1)])
                P.tag = "Bg" + gname + str(l)
                for hp in range(2):
                    for blk in range(2):
                        sl = slice(blk * 512, (blk + 1) * 512)
                        rstd_of(OT[:, hp:hp + 1, sl], 1, rOT[hp][blk * 4:blk * 4 + 4], bd64[:])
                        P.STT('dve', OT[:, hp, sl], OT[:, hp, sl], gn_col(l), RS[:], ALU.mult, ALU.mult,
                              rOT[hp][blk * 4:blk * 4 + 4] + [rPAR, rRS], rOT[hp][blk * 4:blk * 4 + 4])
                for hp in range(2):
                    for blk in range(2):
                        sl = slice(blk * 512, (blk + 1) * 512)
                        b, rb = fproj(wv, rw, 512 + hp * 128, 128, blk)
                        P.ACT(SG[:, hp, sl], b[:, :], AF.Silu, [rb], [rSG])
                        P.TT('pool', MIN[:, 2 + hp, sl], OT[:, hp, sl], SG[:, hp, sl], ALU.mult,
                             rOT[hp][blk * 4:blk * 4 + 4] + [rSG], [rMIN[2 + hp]])

            mods_mm()
            mods_dma()
            P.tag = "C" + gname + (str(l) if "l" in dir() else "")
            if "C" in CFG["mixers"]:
                aAR2.next_phase()
                UG = AR2[:, 0:2048].rearrange("p (g t) -> p g t", g=2)
                VVN = AR2[:, 2048:3072].bitcast(BF16).rearrange("p (t c) -> p t c", t=8)
                WSI = AR2[:, 3072:3584].rearrange("p (g j) -> p g j", g=4)
                WST = AR2[:, 3584:3840].bitcast(BF16)
                BST = AR2[:, 3840:4096].rearrange("p (g i) -> p g i", g=2)
                LNG = AR2[:, 4096:4352]
                rUG, rWSI, rWST, rBST, rLNG = (aAR2.res() for _ in range(5))
                rVVN = [aAR2.res() for _ in range(8)]
                wv, rw = load_w_in(l, 1568, 512)
                P.dma('sp', WSI, W['mlp_ws'][l].rearrange("g i j -> i g j"), wr=[rWSI])
                b, rb = P.bank()
                for g in range(4):
                    P.TR(b[:, g * 128:(g + 1) * 128], WSI[:, g, :], ident[:], [rWSI, rC], [rb], inc=(g == 3))
                P.CP('act', WST, b[:, :], [rb], [rWST])
                for gp in range(2):
                    for hl in range(2):
                        P.dma('sp', BST[hl * 64:(hl + 1) * 64, gp, :], W['mlp_bs'][l, 2 * gp + hl].partition_broadcast(64), wr=[rBST])
                P.dma('sp', LNG, W['mlp_ln'][l].partition_broadcast(128), wr=[rLNG])
                for gp in range(2):
                    for blk in range(2):
                        sl = slice(blk * 512, (blk + 1) * 512)
                        b, rb = fproj(wv, rw, gp * 128, 128, blk)
                        P.ACT(UG[:, gp, sl], b[:, :], AF.Gelu_apprx_tanh, [rb], [rUG])
                aTT.next_phase()
                if CFG.get("c_batched", True):
                    rGVa, rSQa = aTT.res(), aTT.res()
                    GVa = TTB[:, 0:2048].rearrange("p (t c) -> p t c", t=8)
                    SQa = TTB[:, 2048:4096].rearrange("p (t c) -> p t c", t=8)
                    STa = RS[:, 0:32].rearrange("p (a t) -> p a t", a=4)
                    mods_mm()
                    for tt in range(8):
                        b, rb = tproj(wv, rw, 256, 256, tt)
                        P.ACT(GVa[:, tt, :], b[:, 0:256], AF.Gelu_apprx_tanh, [rb], [rGVa])
                    P.op('dve', lambda e: e.reduce_sum(out=STa[:, 0, :], in_=GVa, axis=AX.X), [rGVa], [rRS])
                    P.TS('dve', STa[:, 1, :], STa[:, 0, :], -1.0 / 256, None, ALU.mult, None, [rRS], [rRS])
                    P.TT('dve', GVa, GVa, STa[:, 1, :].unsqueeze(2).to_broadcast([128, 8, 256]), ALU.add, [rGVa, rRS], [rGVa])
                    P.TT('dve', SQa, GVa, GVa, ALU.mult, [rGVa], [rSQa])
                    P.op('dve', lambda e: e.reduce_sum(out=STa[:, 2, :], in_=SQa, axis=AX.X), [rSQa], [rRS])
                    P.ACT(STa[:, 3, :], STa[:, 2, :], AF.Ln, [rRS, rcst], [rRS], bias=epsc, scale=1.0 / 256)
                    P.ACT(STa[:, 3, :], STa[:, 3, :], AF.Exp, [rRS], [rRS], scale=-0.5)
                    P.TT('dve', SQa, GVa, STa[:, 3, :].unsqueeze(2).to_broadcast([128, 8, 256]), ALU.mult, [rGVa, rRS], [rSQa])
                    P.TT('dve', VVN, SQa, LNG.unsqueeze(1).to_broadcast([128, 8, 256]), ALU.mult, [rSQa, rLNG], rVVN)
                    T1a = TTB[:, 0:2048].rearrange("p (t g i) -> p t g i", t=8, g=2)
                    for tt in range(8):
                        for gp in range(2):
                            b, rb = P.bank()
                            P.MM(b[:, 0:256], VVN[:, tt, gp * 128:(gp + 1) * 128], WST[:, gp * 256:(gp + 1) * 256], True, True,
                                 rVVN + [rWST], [rb])
                            for hl in range(2):
                                rows = slice(hl * 64, (hl + 1) * 64)
                                P.TT('dve', T1a[rows, tt, gp, :], b[rows, hl * 128:(hl + 1) * 128], BST[rows, gp, :], ALU.add,
                                     [rb, rBST, rGVa], [rGVa])
                    for gp in range(2):
                        for hl in range(2):
                            rows = slice(hl * 64, (hl + 1) * 64)
                            P.TT('dve', MIN[rows, 4 + gp, :].rearrange("p (t i) -> p t i", t=8), T1a[rows, :, gp, :],
                                 UG[rows, gp, :].rearrange("p (t i) -> p t i", t=8), ALU.mult, [rGVa, rUG], [rMIN[4 + gp]])
                else:
                  cslots = [[aTT.res() for _ in range(5)] for _ in range(4)]

                  def mlp_tile(tt, slot):
                    tsl = slice(tt * 128, (tt + 1) * 128)
                    o_ = slot * 1024
                    GV = TTB[:, o_ + 0:o_ + 256]
                    CEN = TTB[:, o_ + 256:o_ + 512]
                    SQV = TTB[:, o_ + 512:o_ + 768]
                    ST = TTB[:, o_ + 768:o_ + 772]
                    T1 = TTB[:, o_ + 896:o_ + 1024]
                    rGV, rCEN, rSQV, rST, rT1 = cslots[slot]
                    b, rb = tproj(wv, rw, 256, 256, tt)
                    yield
                    P.ACT(GV, b[:, 0:256], AF.Gelu_apprx_tanh, [rb], [rGV])
                    yield
                    P.op('dve', lambda e, o=ST[:, 0:1], i=GV: e.reduce_sum(out=o, in_=i, axis=AX.X), [rGV], [rST])
                    yield
                    P.TS('dve', ST[:, 1:2], ST[:, 0:1], -1.0 / 256, None, ALU.mult, None, [rST], [rST])
                    yield
                    P.TS('dve', CEN, GV, ST[:, 1:2], None, ALU.add, None, [rGV, rST], [rCEN])
                    yield
                    P.TT('dve', SQV, CEN, CEN, ALU.mult, [rCEN], [rSQV])
                    yield
                    P.op('dve', lambda e, o=ST[:, 2:3], i=SQV: e.reduce_sum(out=o, in_=i, axis=AX.X), [rSQV], [rST])
                    yield
                    P.ACT(ST[:, 3:4], ST[:, 2:3], AF.Ln, [rST, rcst], [rST], bias=epsc, scale=1.0 / 256)
                    yield
                    P.ACT(ST[:, 3:4], ST[:, 3:4], AF.Exp, [rST], [rST], scale=-0.5)
                    yield
                    P.STT('dve', VVN[:, tt, :], CEN, ST[:, 3:4], LNG, ALU.mult, ALU.mult, [rCEN, rST, rLNG], [rVVN[tt]])
                    yield
                    for gp in range(2):
                        b, rb = P.bank()
                        P.MM(b[:, 0:256], VVN[:, tt, gp * 128:(gp + 1) * 128], WST[:, gp * 256:(gp + 1) * 256], True, True,
                             [rVVN[tt], rWST], [rb])
                        yield
                        for hl in range(2):
                            rows = slice(hl * 64, (hl + 1) * 64)
                            P.TT('dve', T1[rows, :], b[rows, hl * 128:(hl + 1) * 128], BST[rows, gp, :], ALU.add,
                                 [rb, rBST], [rT1])
                            yield
                            P.TT('gp', MIN[rows, 4 + gp, tsl], T1[rows, :], UG[rows, gp, tsl], ALU.mult,
                                 [rT1, rUG], [rMIN[4 + gp]])
                            yield

                  for grp in ((0, 1, 2, 3), (4, 5, 6, 7)):
                    mods_mm()
                    zip_run([mlp_tile(tt, i) for i, tt in enumerate(grp)])

            mods_mm()
            mods_dma()
            P.tag = "D" + gname + (str(l) if "l" in dir() else "")
            if "D" in CFG["mixers"]:
                nL = 2 * L
                aTT.next_phase()
                HNQ = TMPF[0:1, 0, :]
                PRAW = TTB[:, 0:1024].rearrange("p (k t) -> p k t", k=2)
                rPRAW = [aTT.res(), aTT.res()]
                rHNQ = rTMPF[0]
                rZR, rZS = aTT.res(), aTT.res()
                rFS1a, rFS1b = aTT.res(), aTT.res()
                rFS2a, rFS2b = aTT.res(), aTT.res()
                wv, rw = load_w_in(l, 2080, 768)
                for ci in range(6):
                    for blk in range(2):
                        sl = slice(blk * 512, (blk + 1) * 512)
                        k = (ci * 2 + blk) % 2
                        b, rb = fproj(wv, rw, ci * 128, 128, blk)
                        P.CP('act', PRAW[:, k, :], b[:, :], [rb], [rPRAW[k]])
                        conv3(PTMP[:, ci, sl], PRAW[:, k, :], hyconv_col(l, 0, ci), hyconv_col(l, 1, ci), hyconv_col(l, 2, ci),
                              row_w, [rPRAW[k], rPAR], [rPT[ci]])
                if is_p:
                    CSv = CS256
                    rCS = [rCS256]
                    wcol = wcol256
                    negt = negt256
                else:
                    P.dma('pool', WBR[:].rearrange("p (k n) -> p k n", k=8),
                          C["cs1024"].rearrange("(k p) n -> p k n", p=128), wr=[rWB[0], rWB[1]])
                    CSv = WBR[:].rearrange("p (k n) -> p k n", k=8)
                    rCS = [rWB[0], rWB[1]]
                    wcol = wcol1024
                    negt = negt1024
                P.tag = "Df" + gname + str(l)
                mods_mm()
                aAR2.next_phase()
                HS = AR2[:, 0:2048].bitcast(BF16).rearrange("p (t c) -> p t c", t=8)
                HD = AR2[:, 2048:4096].bitcast(BF16).rearrange("p (t c) -> p t c", t=8)
                HR = AR2[:, 4096:6144].bitcast(BF16).rearrange("p (t c) -> p t c", t=8)
                HI = AR2[:, 6144:8192].bitcast(BF16).rearrange("p (t c) -> p t c", t=8)
                rHS, rHD = aAR2.res(), aAR2.res()
                H1 = AR2[0:64, 4096:5120]
                H2 = AR2[0:64, 5120:6144]
                DEC = AR2[:, 6144:7168]
                HF = AR2[:, 7168:8192]
                rH1, rH2, rDEC, rHF = (aAR2.res() for _ in range(4))
                ARG = TTB[0:64, 0:512]
                M1 = TTB[0:64, 512:1024]
                rARG, rM1 = rPRAW[0], rPRAW[1]
                fcol = freq_col(l)

                def sinlayer(wT, src, rsrc, fb, dst, rdst):
                    n = min(512, L)
                    for c in range(L // n):
                        b, rb = P.bank()
                        P.MM(b[0:64, 0:n], wT, src[:, c * n:(c + 1) * n], True, True, [rSW, rsrc], [rb])
                        P.TS('dve', ARG[:, 0:n], b[0:64, 0:n], fcol, fb, ALU.mult, ALU.add, [rb, rPAR, rSW], [rARG])
                        for _ in range(2):
                            P.TS('dve', M1[:, 0:n], ARG[:, 0:n], -PI, 2 * PI, ALU.is_lt, ALU.mult, [rARG], [rM1])
                            P.TT('dve', ARG[:, 0:n], ARG[:, 0:n], M1[:, 0:n], ALU.add, [rARG, rM1], [rARG])
                            P.TS('dve', M1[:, 0:n], ARG[:, 0:n], PI, -2 * PI, ALU.is_gt, ALU.mult, [rARG], [rM1])
                            P.TT('dve', ARG[:, 0:n], ARG[:, 0:n], M1[:, 0:n], ALU.add, [rARG, rM1], [rARG])
                        P.ACT(dst[:, c * n:(c + 1) * n], ARG[:, 0:n], AF.Sin, [rARG], [rdst])

                FT = AR2[0:17, 5120:5120 + L]
                P.dma('sp', FT, C["feat%d" % L], wr=[rH2])
                EDB = TTB[:, 1024:2048]
                rEDB = [rZR, rZS]
                P.dma('sp', EDB, W['hy_log_decay'][l].rearrange("d o c -> (d o c)").partition_broadcast(128), wr=rEDB)
                P.ACT(EDB, EDB, AF.Exp, rEDB, rEDB)
                sinlayer(HW1[:, l, :], FT, rH2, HFB[:, l, 0:1], H1, rH1)
                sinlayer(HW2[:, l, :], H1[:, 0:L], rH1, HFB[:, l, 1:2], H2, rH2)
                HW3 = TTB[0:64, 0:1024]
                rHW3 = [rARG, rM1]
                P.dma('sp', HW3, W['hy_w3'][l], wr=rHW3)
                for pt in range(ntl):
                    psl = slice(pt * 128, (pt + 1) * 128)
                    P.ACT(DEC, EDB, AF.Exp, rEDB + [rC], [rDEC], scale=negt[:, pt:pt + 1])
                    for hf in range(2):
                        csl = slice(hf * 512, (hf + 1) * 512)
                        b, rb = P.bank()
                        P.MM(b[:, :], H2[:, psl], HW3[:, csl], True, True, [rH2] + rHW3, [rb])
                        P.TT('dve', HF[:, csl], b[:, :], DEC[:, csl], ALU.mult, [rb, rDEC], [rHF])
                    if pt == 0:
                        P.MS('pool', HF[0:1, 512:1024], 0.0, [], [rHF])
                        P.dma('sp', DEC[0:1, 0:512], W['hy_bias'][l:l + 1].rearrange("a o c -> a (o c)"), wr=[rDEC])
                        P.TT('pool', HF[0:1, 0:512], HF[0:1, 0:512], DEC[0:1, 0:512], ALU.add, [rHF, rDEC], [rHF])
                    P.TT('dve', HS[:, pt, :], HF[:, 0:512], HF[:, 512:1024], ALU.add, [rHF], [rHS])
                    P.TT('pool', HD[:, pt, :], HF[:, 512:1024], HF[:, 0:512], ALU.subtract, [rHF], [rHD])
                P.tag = "Dh" + gname + str(l)
                aAR2.next_phase()
                rHS2, rHD2, rHR, rHI = (aAR2.res() for _ in range(4))
                for fc in range(ntl):
                    b, rb = P.bank()
                    P.MMG(b[:, :], [(CSv[:, dc, fc * 128:(fc + 1) * 128], HS[:, dc, :]) for dc in range(ntl)],
                          rCS + [rHS, rHS2], [rb])
                    P.TS('dve', HR[:, fc, :], b[:, :], wcol[:, fc:fc + 1], None, ALU.mult, None, [rb, rC], [rHR])
                    b, rb = P.bank()
                    P.MMG(b[:, :], [(CSv[:, dc, L + fc * 128:L + (fc + 1) * 128], HD[:, dc, :]) for dc in range(ntl)],
                          rCS + [rHD, rHD2], [rb])
                    P.ACT(HI[:, fc, :], b[:, :], AF.Identity, [rb, rC], [rHI], scale=wcol[:, fc:fc + 1])
                b, rb = P.bank()
                P.MMG(b[0:1, :], [(altc[:, 0:1], HS[:, dc, :]) for dc in range(ntl)], [rC, rHS, rHS2], [rb])
                P.TS('dve', HNQ, b[0:1, :], 1.0 / nL, None, ALU.mult, None, [rb], [rHNQ])
                P.tag = "Dl" + gname + str(l)
                aAR2.next_phase()
                rHR2, rHI2 = aAR2.res(), aAR2.res()
                ZT = AR2[:, 0:1024].bitcast(BF16).rearrange("p (t c) -> p t c", t=8)
                Y = AR2[:, 1024:3072].bitcast(BF16).rearrange("p (j c) -> p j c", j=16)
                rZT, rYN = aAR2.res(), aAR2.res()
                rYj = [aAR2.res() for _ in range(16)]
                YN = AR2[0:1, 3072:3584].bitcast(BF16).rearrange("p (s c) -> p s c", s=4)
                fslots = [(TTB[:, 0:1024], rPRAW[0], rPRAW[1]), (TTB[:, 1024:2048], rZR, rZS),
                          (TTB[:, 2048:3072], rFS1a, rFS1b), (TTB[:, 3072:4096], rFS2a, rFS2b)]
                nseq = len(seqs)
                for o in range(2):
                    osl = slice(o * 256, (o + 1) * 256)
                    for tt in range(8):
                        b, rb = P.bank()
                        for cc in range(2):
                            P.TR(b[:, cc * 128:(cc + 1) * 128], PTMP[:, 4 + cc, tt * 128:(tt + 1) * 128], ident[:],
                                 [rPT[4 + cc], rC], [rb], inc=(cc == 1))
                        P.EV(ZT[:, tt, :], b[:, 0:256], [rb], [rZT])

                    def fchain(si, fcx, slot, osl=osl):
                        t0, nt = seqs[si]
                        yb = si * 2 * nt
                        TS_, rTa, rTb = fslots[slot]
                        T1, T2, T3, T4 = (TS_[:, i * 256:(i + 1) * 256] for i in range(4))
                        b, rb = P.bank()
                        P.MMG(b[:, 0:256], [(CSv[:, tl, fcx * 128:(fcx + 1) * 128], ZT[:, t0 + tl, :]) for tl in range(nt)],
                              rCS + [rZT], [rb])
                        P.MMG(b[:, 256:512], [(CSv[:, tl, L + fcx * 128:L + (fcx + 1) * 128], ZT[:, t0 + tl, :]) for tl in range(nt)],
                              rCS + [rZT], [rb])
                        yield
                        P.TT('dve', T1, b[:, 0:256], HR[:, fcx, osl], ALU.mult, [rb, rHR, rHR2], [rTa])
                        P.TT('dve', T2, b[:, 256:512], HI[:, fcx, osl], ALU.mult, [rb, rHI, rHI2], [rTa])
                        P.TT('dve', T3, b[:, 256:512], HR[:, fcx, osl], ALU.mult, [rb, rHR, rHR2], [rTb])
                        P.TT('dve', T4, b[:, 0:256], HI[:, fcx, osl], ALU.mult, [rb, rHI, rHI2], [rTb])
                        yield
                        P.TT('gp', Y[:, yb + fcx, :], T1, T2, ALU.add, [rTa], [rYj[yb + fcx]])
                        P.TT('gp', Y[:, yb + nt + fcx, :], T3, T4, ALU.subtract, [rTb], [rYj[yb + nt + fcx]])
                        yield

                    chains = [(si, fcx) for si in range(nseq) for fcx in range(seqs[si][1])]
                    for c0 in range(0, len(chains), 4):
                        zip_run([fchain(si, fcx, i) for i, (si, fcx) in enumerate(chains[c0:c0 + 4])])
                    for si, (t0, nt) in enumerate(seqs):
                        b, rb = P.bank()
                        P.MMG(b[0:1, 0:256], [(altc[:, 0:1], ZT[:, t0 + tl, :]) for tl in range(nt)], [rC, rZT], [rb])
                        P.TT('dve', YN[0:1, si, :], b[0:1, 0:256], HNQ[0:1, osl], ALU.mult, [rb, rHNQ], [rYN])
                    for si, (t0, nt) in enumerate(seqs):
                        yb = si * 2 * nt
                        n = min(512, L)
                        for cc in range(2):
                            csl = slice(cc * 128, (cc + 1) * 128)
                            for tb in range(L // n):
                                b, rb = P.bank()
                                pairs = []
                                for j in range(2 * nt):
                                    off = (L if j >= nt else 0) + tb * n
                                    pairs.append((Y[:, yb + j, csl], CSv[:, j % nt, off:off + n]))
                                pairs.append((YN[0:1, si, csl], altr[0:1, tb * n:tb * n + n]))
                                P.MMG(b[:, 0:n], pairs, rCS + rYj[yb:yb + 2 * nt] + [rYN, rC], [rb])
                                tok = slice(t0 * 128 + tb * n, t0 * 128 + tb * n + n)
                                if o == 0:
                                    P.TT('dve', PTMP[:, 4 + cc, tok], PTMP[:, cc, tok], b[:, 0:n], ALU.mult,
                                         [rPT[cc], rb, rPT[4 + cc]], [rPT[4 + cc]])
                                else:
                                    P.TT('dve', MIN[:, 6 + cc, tok], PTMP[:, 2 + cc, tok], b[:, 0:n], ALU.mult,
                                         [rPT[2 + cc], rb], [rMIN[6 + cc]])

            mods_mm()
            mods_dma()
            P.tag = "WO" + gname + (str(l) if "l" in dir() else "")
            mods_until(l, 2)
            aAR2.next_phase()
            MT = AR2[:, 0:4096].rearrange("p (c t) -> p c t", c=8)
            rMT = aAR2.res()
            wb, rw = wbuf()
            wv = wb[:, 0:8192].rearrange("p (k n) -> p k n", k=8)
            P.dma('pool', wv, W['w_out'][l].rearrange("(k p) n -> p k n", p=128), wr=[rw])
            after_load()
            own = (not is_p) and (l == CFG["layers"] - 1) and CFG.get("own", True)
            if own:
                P.dma_fn('sp', lambda e: e.dma_start(out=MIN[:, :, 0:256], in_=MIN[:, :, bass.ds(dyn['c'], 256)]), rMIN, rMIN)
                P.dma_fn('sp', lambda e: e.dma_start(out=XT[:, :, 0:256], in_=XT[:, :, bass.ds(dyn['c'], 256)]), rXT, rXT)
                blks = [(0, slice(0, 256))]
            else:
                blks = [(0, slice(0, 512)), (1, slice(512, 1024))]
            for blk, sl in blks:
                n = sl.stop - sl.start
                for m in range(8):
                    b, rb = P.bank()
                    P.MMG(b[:, 0:n], [(wv[:, kc, m * 128:(m + 1) * 128], MIN[:, kc, sl]) for kc in range(8)], [rw] + rMIN, [rb])
                    P.EV(MT[:, m, 0:n], b[:, 0:n], [rb], [rMT])
                rstd_of(MT[:, :, 0:n], 8, [rMT], onesd[:], n)
                for fc in range(8):
                    P.TT('dve', TMPF[:, fc % 2, 0:n], MT[:, fc, 0:n], RS[:, 0:n], ALU.mult, [rMT, rRS], [rTMPF[fc % 2]])
                    P.STT('pool', XT[:, fc, sl], TMPF[:, fc % 2, 0:n], GG[:, l, 2, fc, v:v + 1], XT[:, fc, sl], ALU.mult, ALU.add,
                          [rTMPF[fc % 2], rGGw[l][2], rXT[blk]], [rXT[blk]])
                norm_mod(l, blk, 1, 24, v, sl)

            mods_mm()
            mods_dma()
            P.tag = "FFN" + gname + (str(l) if "l" in dir() else "")
            aBIG.next_phase()
            AT = BIG[:, :].bitcast(BF16).rearrange("p (c t) -> p c t", c=22)
            rAT = [[aBIG.res() for _ in range(22)] for _ in range(2)]
            aTT.next_phase()
            SL = TTB[:, 0:1024].rearrange("p (k t) -> p k t", k=2)
            rSL = [aTT.res(), aTT.res()]
            cnt = 0
            cbs = [(i * 512, 512) for i in range(5)] + [(2560, 256)]
            for cbi, (c0, cw) in enumerate(cbs):
                if cbi == 1:
                    mods_mm()
                wb, rw = wbuf()
                w1v = wb[:, 0:8 * cw].rearrange("p (k n) -> p k n", k=8)
                w3v = wb[:, 4096:4096 + 8 * cw].rearrange("p (k n) -> p k n", k=8)
                P.dma('pool', w1v, W['ffn_w1'][l][:, c0:c0 + cw].rearrange("(k p) n -> p k n", p=128), wr=[rw])
                P.dma('pool', w3v, W['ffn_w3'][l][:, c0:c0 + cw].rearrange("(k p) n -> p k n", p=128), wr=[rw])
                after_load()
                order = [(j, bs) for bs in blks for j in range(cw // 128)] if cbi == 0 else [(j, bs) for j in range(cw // 128) for bs in blks]
                for j, (blk, sl) in order:
                    n = sl.stop - sl.start
                    b1, rb1 = fproj(w1v, rw, j * 128, 128, blk, sl)
                    b3, rb3 = fproj(w3v, rw, j * 128, 128, blk, sl)
                    k = cnt % 2
                    cnt += 1
                    P.ACT(SL[:, k, 0:n], b1[:, 0:n], AF.Silu, [rb1], [rSL[k]])
                    P.TT('dve', AT[:, c0 // 128 + j, sl], SL[:, k, 0:n], b3[:, 0:n], ALU.mult, [rSL[k], rb3], [rAT[blk][c0 // 128 + j]])
            mods_until(l, 5)
            aAR2.next_phase()
            MT = AR2[:, 0:8192].rearrange("p (c t) -> p c t", c=8)
            rMT2 = [aAR2.res(), aAR2.res()]
            for db in range(4):
                wb, rw = wbuf()
                w2v = wb[:, 0:22 * 256].rearrange("p (k n) -> p k n", k=22)
                P.dma('pool', w2v, W['ffn_w2'][l][:, db * 256:(db + 1) * 256].rearrange("(k p) n -> p k n", p=128), wr=[rw])
                def w2_group(m, blk, sl):
                    n = sl.stop - sl.start
                    b, rb = P.bank()
                    P.MMG(b[:, 0:n], [(w2v[:, fc, m * 128:(m + 1) * 128], AT[:, fc, sl]) for fc in range(22)], [rw] + rAT[blk], [rb])
                    P.EV(MT[:, 2 * db + m, sl], b[:, 0:n], [rb], [rMT2[blk]])

                def w2_post(blk, sl):
                    n = sl.stop - sl.start
                    rstd_of(MT[:, :, sl], 8, [rMT2[blk]], onesd[:], n)
                    for fc in range(8):
                        P.TT('dve', TMPF[:, fc % 2, 0:n], MT[:, fc, sl], RS[:, 0:n], ALU.mult, [rMT2[blk], rRS], [rTMPF[fc % 2]])
                        P.STT('pool', XT[:, fc, sl], TMPF[:, fc % 2, 0:n], GG[:, l, 3, fc, v:v + 1], XT[:, fc, sl], ALU.mult, ALU.add,
                              [rTMPF[fc % 2], rGGw[l][3], rXT[blk]], [rXT[blk]])

                if db < 3:
                    for m in range(2):
                        for blk, sl in blks:
                            w2_group(m, blk, sl)
                else:
                    for blk, sl in blks:
                        for m in range(2):
                            w2_group(m, blk, sl)
                        w2_post(blk, sl)

        P.tag = "OUT" + gname + (str(l) if "l" in dir() else "")
        aTT.next_phase()
        rXS = [aTT.res("XS0"), aTT.res("XS1")]
        own_out = (not is_p) and CFG.get("own", True)
        if own_out:
            y_d = ysq_d
        for tt in range(2 if own_out else 8):
            k = tt % 2
            for half in range(2):
                b, rb = P.bank()
                for j in range(4):
                    P.TR(b[:, j * 128:(j + 1) * 128], XT[:, half * 4 + j, tt * 128:(tt + 1) * 128], ident[:],
                         [rXT[tt // 4], rC], [rb], inc=(j == 3))
                P.EV(XS[:, k, half * 512:(half + 1) * 512], b[:, :], [rb], [rXS[k]])
            P.dma('sp', y_d[tt * 128:(tt + 1) * 128, :], XS[:, k, :], rd=[rXS[k]], is_out=True)

    for g in CFG["groups"]:
        run_group(g)

    with nc.allow_non_contiguous_dma(reason="small strided parameter loads"):
        P.emit()
    return nc, P


_CACHE = {}


def kernel(**inputs):
    n = 8
    inp = {k: np.ascontiguousarray(np.asarray(v, dtype=np.float32)) for k, v in inputs.items()}
    consts = host_consts()
    if "nc" not in _CACHE:
        _CACHE["nc"] = build()
    nc, P = _CACHE["nc"]
    in_maps = []
    for r in range(n):
        b = r // 4
        m = {
            "xp": np.ascontiguousarray(inp["x_prompt"][4 * r:4 * r + 4].reshape(NTOK, D)),
            "xs": np.ascontiguousarray(inp["x_sample"][b]),
            "s0": np.ascontiguousarray(inp["state_gla"][b]),
            "cv": np.ascontiguousarray(np.stack([inp["c_ctx"], inp["c"][b]], axis=0)),
            "qoff": np.array([[256 * (r % 4)]], np.int32),
        }
        for k in WEIGHT_SHAPES:
            m[k] = inp[k]
        for k, vv in consts.items():
            m["c_" + k] = vv
        in_maps.append(m)
    res = run_bass_kernel_spmd(nc, in_maps, core_ids=list(range(n)))
    y_prompt = np.zeros((32, 256, D), np.float32)
    y_sample = np.zeros((2, 1024, D), np.float32)
    new_state = np.zeros((32, 2, 2, 4, 32, 64), np.float32)
    for r in range(n):
        o = res.results[r]
        y_prompt[4 * r:4 * r + 4] = np.asarray(o["yp"]).reshape(4, 256, D)
        new_state[4 * r:4 * r + 4] = np.asarray(o["ns"]).reshape(4, 2, 2, 4, 32, 64)
        if CFG.get("own", True):
            y_sample[r // 4, 256 * (r % 4):256 * (r % 4) + 256] = np.asarray(o["ysq"])
        elif r % 4 == 0:
            y_sample[r // 4] = np.asarray(o["ys"])
    return (y_prompt, y_sample, new_state)
```

```python
import math
import numpy as np
from contextlib import ExitStack
import concourse.bass as bass
import concourse.mybir as mybir
from concourse.bass_utils import run_bass_kernel_spmd

F32 = mybir.dt.float32
BF16 = mybir.dt.bfloat16
AF = mybir.ActivationFunctionType
ALU = mybir.AluOpType
AX = mybir.AxisListType

ENGS = ['pe', 'act', 'dve', 'pool', 'sp']
NDSEM = 10
PI = math.pi


class Res:
    __slots__ = ('name', 'w', 'r', 'excl')

    def __init__(self, name):
        self.name = name
        self.w = None
        self.r = []
        self.excl = False


class Op:
    __slots__ = ('eng', 'fn', 'kind', 'deps', 'seq', 'inc', 'dsem', 'dval', 'prev', 'idx', 'tag')

    def __init__(self, eng, fn, kind, inc):
        self.eng = eng
        self.fn = fn
        self.kind = kind
        self.inc = inc
        self.deps = []
        self.seq = None
        self.dsem = None
        self.dval = None
        self.prev = None
        self.idx = 0


class Arena:
    def __init__(self):
        self.cur = []
        self.hist = {}
        self.hist_dma = []

    def next_phase(self):
        for r in self.cur:
            ops = list(r.r)
            if r.w is not None:
                ops.append(r.w)
            for o in ops:
                if o.kind == 'd':
                    self.hist_dma.append(o)
                else:
                    p = self.hist.get(o.eng)
                    if p is None or o.idx > p.idx:
                        self.hist[o.eng] = o
        self.cur = []
        self.hist_dma = self.hist_dma[-64:]

    def res(self, name="a"):
        r = Res(name)
        r.r = list(self.hist.values()) + list(self.hist_dma)
        self.cur.append(r)
        return r


class Prog:
    def __init__(self, nc):
        self.nc = nc
        self.es = ExitStack()
        self.ops = {e: [] for e in ENGS}
        self.ndma = {e: 0 for e in ENGS}
        self.esem = {e: self.es.enter_context(nc.semaphore("es_" + e)) for e in ENGS}
        self.dsem = {e: [self.es.enter_context(nc.semaphore("ds_%s%d" % (e, i))) for i in range(NDSEM)]
                     for e in ('sp', 'pool', 'act')}
        self.dlast = {e: [None] * NDSEM for e in ('sp', 'pool', 'act')}
        self.out_dmas = []
        self.nres = 0
        self.banks = []
        self.bank_i = 0
        self.flip = 0

    def sb(self, name, shape, dtype=F32):
        return self.es.enter_context(self.nc.sbuf_tensor(name, list(shape), dtype))

    def res(self, name=None):
        self.nres += 1
        return Res(name or ("r%d" % self.nres))

    def make_banks(self):
        for i in range(8):
            t = self.es.enter_context(self.nc.psum_tensor("pb%d" % i, [128, 512], F32))
            r = self.res("pb%d" % i)
            r.excl = True
            self.banks.append((t, r))

    def bank(self):
        nb = 7 if CFG.get("warm", 0) else 8
        b = self.banks[self.bank_i % nb]
        self.bank_i += 1
        return b

    def warm(self, n=None):
        n = CFG.get("warm", 0) if n is None else n
        if not n:
            return
        t, r = self.banks[7]
        lhs, rhs = self.warm_ops
        for _ in range(n):
            self.op('pe', lambda e: e.matmul(t[:, :], lhsT=lhs, rhs=rhs, start=True, stop=True), [], [], inc=False)

    def _add(self, o, rd, wr):
        deps = []
        for r in rd:
            if r.w is not None:
                deps.append((r.w, 'raw', r.name))
            if r.excl:
                for x in r.r:
                    if x.eng != o.eng:
                        deps.append((x, 'rar', r.name))
        for w in wr:
            if w.w is not None:
                deps.append((w.w, 'waw', w.name))
            for x in w.r:
                deps.append((x, 'war', w.name))
        for r in rd:
            r.r.append(o)
        for w in wr:
            w.w = o
            w.r = []
        o.deps = [(d, t, nm) for (d, t, nm) in deps if d is not o]
        o.idx = len(self.ops[o.eng])
        o.tag = getattr(self, 'tag', '')
        self.ops[o.eng].append(o)
        return o

    def op(self, eng, fn, rd=(), wr=(), inc=True):
        if eng == 'pool':
            eng = 'dve'
        if eng == 'gp':
            eng = 'pool' if CFG.get("use_pool", False) else 'dve'
        return self._add(Op(eng, fn, 'c', inc), rd, wr)

    def dma(self, eng, out, in_, rd=(), wr=(), is_out=False):
        o = Op(eng, lambda e: e.dma_start(out=out, in_=in_), 'd', True)
        n = self.ndma[eng]
        self.ndma[eng] = n + 1
        s = n % NDSEM
        o.dsem = self.dsem[eng][s]
        o.dval = 16 * (n // NDSEM + 1)
        o.prev = self.dlast[eng][s]
        self.dlast[eng][s] = o
        self._add(o, rd, wr)
        if is_out:
            self.out_dmas.append(o)
        return o

    def dma_fn(self, eng, fn, rd=(), wr=(), is_out=False):
        o = Op(eng, fn, 'd', True)
        n = self.ndma[eng]
        self.ndma[eng] = n + 1
        s = n % NDSEM
        o.dsem = self.dsem[eng][s]
        o.dval = 16 * (n // NDSEM + 1)
        o.prev = self.dlast[eng][s]
        self.dlast[eng][s] = o
        self._add(o, rd, wr)
        if is_out:
            self.out_dmas.append(o)
        return o

    def MM(self, out, lhsT, rhs, start, stop, rd, wr, inc=None):
        if inc is None:
            inc = stop
        return self.op('pe', lambda e: e.matmul(out, lhsT=lhsT, rhs=rhs, start=start, stop=stop), rd, wr, inc)

    def MMG(self, out, pairs, rd, wr):
        n = len(pairs)
        for i, (l, r) in enumerate(pairs):
            self.MM(out, l, r, i == 0, i == n - 1, rd, wr)

    def TR(self, out, in_, ident, rd, wr, inc=True):
        return self.op('pe', lambda e: e.transpose(out, in_, ident), rd, wr, inc)

    def ACT(self, out, in_, func, rd, wr, bias=None, scale=None):
        kw = {}
        if bias is not None:
            kw['bias'] = bias
        if scale is not None:
            kw['scale'] = scale
        return self.op('act', lambda e: e.activation(out=out, in_=in_, func=func, **kw), rd, wr)

    def CP(self, eng, out, in_, rd, wr):
        if eng == 'act':
            return self.op('act', lambda e: e.copy(out=out, in_=in_), rd, wr)
        return self.op(eng, lambda e: e.tensor_copy(out=out, in_=in_), rd, wr)

    def EV(self, out, in_, rd, wr):
        self.flip = (self.flip + 1) % 4
        return self.CP('act', out, in_, rd, wr)

    def TT(self, eng, out, in0, in1, op, rd, wr):
        return self.op(eng, lambda e: e.tensor_tensor(out=out, in0=in0, in1=in1, op=op), rd, wr)

    def TS(self, eng, out, in0, s1, s2, op0, op1, rd, wr):
        if s2 is None:
            return self.op(eng, lambda e: e.tensor_scalar(out=out, in0=in0, scalar1=s1, scalar2=None, op0=op0), rd, wr)
        return self.op(eng, lambda e: e.tensor_scalar(out=out, in0=in0, scalar1=s1, scalar2=s2, op0=op0, op1=op1), rd, wr)

    def STT(self, eng, out, in0, scalar, in1, op0, op1, rd, wr):
        eng = 'dve'
        return self.op(eng, lambda e: e.scalar_tensor_tensor(out=out, in0=in0, scalar=scalar, in1=in1, op0=op0, op1=op1), rd, wr)

    def MS(self, eng, out, val, rd, wr):
        return self.op(eng, lambda e: e.memset(out, val), rd, wr)

    def emit(self):
        nc = self.nc
        fin = Op('sp', None, 'c', False)
        fin.deps = [(d, 'raw', 'out') for d in self.out_dmas]
        self.ops['sp'].append(fin)
        for e in ENGS:
            cnt = 0
            pend = []
            lst = self.ops[e]
            last_c = None
            for o in lst:
                if o.kind == 'c' and o.fn is not None:
                    last_c = o
            if last_c is not None and e != 'sp':
                last_c.inc = True
            for o in lst:
                if o.kind != 'c' or o.fn is None:
                    continue
                if o.inc:
                    cnt += 1
                    o.seq = cnt
                    for p in pend:
                        p.seq = cnt
                    pend = []
                else:
                    pend.append(o)
            assert not pend or e == 'sp'
        self.stats = {e: len(self.ops[e]) for e in ENGS}
        self.selfwaits = {}

        def run(e, eng):
            waited = {}
            esem = self.esem

            def wait(sem, val):
                k = id(sem)
                if waited.get(k, 0) >= val:
                    return False
                waited[k] = val
                eng.wait_ge(sem, val)
                return True

            for o in self.ops[e]:
                for (d, t, nm) in o.deps:
                    if d.kind == 'd':
                        wait(d.dsem, d.dval)
                    else:
                        if d.eng == e:
                            if e == 'pe':
                                continue
                            if o.kind == 'c' and t != 'raw' and not CFG.get("strict_same_engine", True):
                                continue
                        if wait(esem[d.eng], d.seq) and d.eng == e:
                            key = (e, t, nm)
                            self.selfwaits[key] = self.selfwaits.get(key, 0) + 1
                if o.fn is None:
                    continue
                if o.kind == 'd':
                    if o.prev is not None:
                        wait(o.dsem, o.prev.dval)
                    o.fn(eng).then_inc(o.dsem, 16)
                else:
                    ins = o.fn(eng)
                    if o.inc:
                        ins.then_inc(esem[e], 1)

        with nc.Block() as block:
            @block.tensor
            def _(eng):
                run('pe', eng)

            @block.scalar
            def _(eng):
                run('act', eng)

            @block.vector
            def _(eng):
                run('dve', eng)

            @block.gpsimd
            def _(eng):
                run('pool', eng)

            @block.sync
            def _(eng):
                run('sp', eng)
        self.es.close()


D = 1024
DEPTH = 2
NTOK = 1024
IN_COLS = 2848
D_FF = 2816
EPS = 1e-6
CFG = {"mixers": "ABCD", "groups": "PS", "layers": 2, "strict_same_engine": False}


_WARM = [None]


def zip_run(gens):
    gens = list(gens)
    while gens:
        for g in list(gens):
            try:
                next(g)
            except StopIteration:
                gens.remove(g)
        if _WARM[0] is not None:
            _WARM[0]()


def host_consts():
    c = {}
    c["ident"] = np.eye(128, dtype=np.float32)
    j = np.arange(128)[:, None]
    i = np.arange(128)[None, :]
    tri = np.zeros((128, 4, 128), np.float32)
    tri[:, 0, :] = (j <= i) * (-1.0 / 16)
    tri[:, 1, :] = (j > i) * (-1.0 / 16)
    tri[:, 2, :] = (j >= i) * (-1.0 / 16)
    tri[:, 3, :] = (j < i) * (-1.0 / 16)
    c["tri"] = tri
    cm = np.zeros((128, 2, 128), np.float32)
    cm[:, 0, :] = (j <= i)
    cm[:, 1, :] = (j >= i)
    c["cmask"] = cm
    hm = np.zeros((128, 4, 128), np.float32)
    for h in range(4):
        hm[32 * h:32 * h + 32, h, :] = 1.0
    c["hm4"] = hm
    bd = np.zeros((128, 128), np.float32)
    bd[:64, :64] = 1.0 / 64
    bd[64:, 64:] = 1.0 / 64
    c["bd64"] = bd
    c["onesd"] = np.full((128, 128), 1.0 / 1024, np.float32)
    alt = np.where(np.arange(1024) % 2 == 0, 1.0, -1.0).astype(np.float32)
    c["altr"] = alt[None, :].copy()
    c["altc"] = alt[:128, None].copy()
    c["onesr"] = np.ones((1, 128), np.float32)
    for L in (256, 1024):
        n = 2 * L
        t = np.arange(L, dtype=np.float64)
        th = 2.0 * np.pi * np.outer(t, t) / n
        c["cs%d" % L] = np.concatenate([np.cos(th), np.sin(th)], axis=1).astype(np.float32)
        w = np.full((128, L // 128), 2.0 / n, np.float32)
        w[0, 0] = 1.0 / n
        c["wcol%d" % L] = w
        pos = np.arange(L, dtype=np.float32)
        tt = pos / np.float32(max(L - 1, 1))
        omega = (np.float32(2.0 * math.pi) * pos / np.float32(L)).astype(np.float32)
        bands = np.linspace(1e-4, 7, 8, dtype=np.float32)
        feat = np.concatenate([tt[:, None], np.cos(omega[:, None] * bands), np.sin(omega[:, None] * bands)], axis=-1)
        c["feat%d" % L] = np.ascontiguousarray(feat.T.astype(np.float32))
        c["negt%d" % L] = np.ascontiguousarray((-tt).reshape(L // 128, 128).T.astype(np.float32))
    return c


CONST_SHAPES = {k: v.shape for k, v in host_consts().items()}

WEIGHT_SHAPES = {
    'w_mod': (2, 1024, 6144), 'b_mod': (2, 6144), 'g_pre1': (2, 1024), 'g_post1': (2, 1024),
    'g_pre2': (2, 1024), 'g_post2': (2, 1024), 'w_in': (2, 1024, 2848), 'conv_a': (2, 3, 256),
    'gla_wgk': (2, 2, 16, 128), 'gla_bgk': (2, 2, 128), 'gla_gnorm': (2, 64), 'mlp_ln': (2, 256),
    'mlp_ws': (2, 4, 128, 128), 'mlp_bs': (2, 4, 128), 'hy_conv': (2, 3, 768), 'hy_w1': (2, 17, 64),
    'hy_b1': (2, 64), 'hy_freq': (2, 64), 'hy_w2': (2, 64, 64), 'hy_b2': (2, 64), 'hy_w3': (2, 64, 1024),
    'hy_log_decay': (2, 2, 2, 256), 'hy_bias': (2, 2, 256), 'w_out': (2, 1024, 1024),
    'ffn_w1': (2, 1024, 2816), 'ffn_w3': (2, 1024, 2816), 'ffn_w2': (2, 2816, 1024),
}


def build():
    nc = bass.Bass("TRN2", target_bir_lowering=False)

    def din(name, shape):
        return nc.dram_tensor(name, list(shape), F32, kind="ExternalInput").ap()

    def dout(name, shape):
        return nc.dram_tensor(name, list(shape), F32, kind="ExternalOutput").ap()

    xp_d = din("xp", (NTOK, D))
    xs_d = din("xs", (NTOK, D))
    s0_d = din("s0", (2, 2, 4, 32, 64))
    cv_d = din("cv", (2, D))
    W = {k: din(k, s) for k, s in WEIGHT_SHAPES.items()}
    C = {k: din("c_" + k, s) for k, s in CONST_SHAPES.items()}
    yp_d = dout("yp", (NTOK, D))
    ys_d = dout("ys", (NTOK, D)) if not CFG.get("own", True) else None
    ysq_d = dout("ysq", (256, D))
    qoff_d = nc.dram_tensor("qoff", [1, 1], mybir.dt.int32, kind="ExternalInput").ap()
    dyn = {}
    ns_d = dout("ns", (4, 2, 2, 128, 64))

    P = Prog(nc)
    P.make_banks()
    sb = P.sb

    def _setup_q(e):
        reg = e.alloc_register("qoff")
        e.reg_load(reg, qoff_d[0:1, 0:1])
        dyn['c'] = e.snap(reg, min_val=0, max_val=768)
        return None

    P.op('sp', _setup_q, inc=False)

    XT = sb("XT", [128, 8, NTOK], F32)
    rXT = [P.res("XT0"), P.res("XT1")]
    HT = sb("HT", [128, 8, NTOK], BF16)
    rHT = [[P.res("HT%d_%d" % (b_, f_)) for f_ in range(8)] for b_ in range(2)]
    BIG = sb("BIG", [128, 11264], F32)
    aBIG = Arena()
    AR2 = sb("AR2", [128, 8192], F32)
    aAR2 = Arena()
    WBR = sb("WBR", [128, 16384], BF16)
    rWB = [P.res("WB0"), P.res("WB1")]
    SQ = sb("SQ", [128, 8, 512], BF16)
    rSQ = P.res("SQ")
    rSQc = [P.res("SQ%d" % i) for i in range(4)]
    RS = sb("RS", [128, 512], F32)
    rRS = P.res("RS")
    TMPF = sb("TMPF", [128, 2, 512], F32)
    rTMPF = [P.res("TMPF0"), P.res("TMPF1")]
    TTB = sb("TTB", [128, 4096], F32)
    aTT = Arena()
    XS = TTB[:, 0:2048].rearrange("p (k d) -> p k d", k=2)
    CS256 = sb("CS256", [128, 2, 512], BF16)
    rCS256 = P.res()
    ident = sb("ident", [128, 128], F32)
    tri = sb("tri", [128, 4, 128], F32)
    cmask = sb("cmask", [128, 2, 128], F32)
    hm4 = sb("hm4", [128, 4, 128], F32)
    bd64 = sb("bd64", [128, 128], BF16)
    onesd = sb("onesd", [128, 128], BF16)
    altr = sb("altr", [1, 1024], BF16)
    altc = sb("altc", [128, 1], BF16)
    onesr = sb("onesr", [1, 128], BF16)
    wcol256 = sb("wcol256", [128, 2], F32)
    wcol1024 = sb("wcol1024", [128, 8], F32)
    negt256 = sb("negt256", [128, 2], F32)
    negt1024 = sb("negt1024", [128, 8], F32)
    rC = P.res("consts")
    cst = sb("cst", [128, 8], F32)
    rcst = P.res("cst")
    PST = AR2[:, 0:384].rearrange("p (a b) -> p a b", a=3)
    rPST = aAR2.res("PST")
    PAR = sb("PAR", [128, 3, 128], F32)
    rPAR = P.res("PAR")
    MOD = sb("MOD", [128, 2, 48, 2], F32)
    rMOD = P.res("MOD")
    GG = sb("GG", [128, 2, 4, 8, 2], F32)
    rGG = P.res("GG")
    SCB = sb("SCB", [128, 8, 2], BF16)
    rSCB = P.res("SCB")
    WG = sb("WG", [32, 2, 2, 128], BF16)
    BGK = sb("BGK", [1, 2, 2, 128], BF16)
    HW1 = sb("HW1", [17, 2, 64], F32)
    HW2 = sb("HW2", [64, 2, 64], F32)
    HFB = sb("HFB", [64, 2, 2], F32)
    rSW = P.res("smallw")
    SST = sb("SST", [128, 4, 256], F32)
    rSST = [P.res("S%d" % i) for i in range(4)]
    SBF = sb("SBF", [128, 2, 256], BF16)
    rSBF = [P.res("SBF0"), P.res("SBF1")]
    SCMP = sb("SCMP", [128, 2, 64], F32)
    rSCMP = [P.res("SC0"), P.res("SC1")]

    P.warm_ops = (onesd[:], CS256[:, 0, :])
    _WARM[0] = P.warm
    WB = [WBR[:, 0:8192], WBR[:, 8192:16384]]
    wbi = [0]

    mods_pending = [None]

    def wbuf():
        if mods_pending[0] is not None:
            if mods_pending[0] >= 1:
                next(_mgh[0])
            else:
                mods_pending[0] += 1
        k = wbi[0] % 2
        wbi[0] += 1
        return WB[k], rWB[k]

    for (t, k) in ((ident, "ident"), (tri, "tri"), (cmask, "cmask"), (hm4, "hm4"), (wcol256, "wcol256"),
                   (wcol1024, "wcol1024"), (negt256, "negt256"), (negt1024, "negt1024")):
        P.dma('sp', t[:], C[k], wr=[rC])
    for (t, k) in ((bd64, "bd64"), (onesd, "onesd"), (altr, "altr"), (altc, "altc"), (onesr, "onesr")):
        P.dma('pool', t[:], C[k], wr=[rC])
    P.dma('pool', CS256[:], C["cs256"].rearrange("(tc p) n -> p tc n", p=128), wr=[rCS256])
    P.MS('dve', cst[:, 0:1], EPS, [], [rcst])
    P.MS('dve', cst[:, 1:2], -PI, [], [rcst])
    P.MS('dve', cst[:, 2:3], 1.0, [], [rcst])
    P.MS('dve', cst[:, 3:4], 0.0, [], [rcst])
    epsc = cst[:, 0:1]

    P.MS('pool', PST[:], 0.0, [], [rPST])
    for l in range(2):
        P.dma('sp', PST[l * 48:(l + 1) * 48, 0, :], W['b_mod'][l].rearrange("(c p) -> c p", p=128), wr=[rPST])
    for l in range(2):
        for wi, nm in enumerate(('g_pre1', 'g_post1', 'g_pre2', 'g_post2')):
            r0 = l * 32 + wi * 8
            P.dma('sp', PST[r0:r0 + 8, 1, :], W[nm][l].rearrange("(c p) -> c p", p=128), wr=[rPST])
    for v in range(2):
        P.dma('sp', PST[64 + v * 8:72 + v * 8, 1, :], cv_d[v].rearrange("(c p) -> c p", p=128), wr=[rPST])
    for l in range(2):
        P.dma('sp', PST[80 + l * 6:86 + l * 6, 1, :], W['conv_a'][l].rearrange("k (c p) -> (k c) p", p=128), wr=[rPST])
        P.dma('sp', PST[92 + l:93 + l, 1, 0:64], W['gla_gnorm'][l:l + 1, :], wr=[rPST])
        P.dma('sp', PST[92 + l:93 + l, 1, 64:128], W['gla_gnorm'][l:l + 1, :], wr=[rPST])
        P.dma('sp', PST[94 + l:95 + l, 1, 0:64], W['hy_freq'][l:l + 1, :], wr=[rPST])
        P.dma('sp', PST[96 + l:97 + l, 1, 0:64], W['hy_b1'][l:l + 1, :], wr=[rPST])
        P.dma('sp', PST[98 + l:99 + l, 1, 0:64], W['hy_b2'][l:l + 1, :], wr=[rPST])
        P.dma('sp', PST[l * 18:(l + 1) * 18, 2, :], W['hy_conv'][l].rearrange("k (c p) -> (k c) p", p=128), wr=[rPST])
    for i in range(3):
        b, rb = P.bank()
        P.TR(b[:, 0:128], PST[:, i, :], ident[:], [rPST, rC], [rb])
        P.EV(PAR[:, i, :], b[:, 0:128], [rb], [rPAR])

    def bmod_col(l, c0, c1):
        return PAR[:, 0, l * 48 + c0:l * 48 + c1]

    def gv_col(l, wi):
        return PAR[:, 1, l * 32 + wi * 8:l * 32 + wi * 8 + 8]

    def conva_col(l, k, ch):
        c = 80 + l * 6 + k * 2 + ch
        return PAR[:, 1, c:c + 1]

    def hyconv_col(l, k, ci):
        c = l * 18 + k * 6 + ci
        return PAR[:, 2, c:c + 1]

    def gn_col(l):
        return PAR[:, 1, 92 + l:93 + l]

    def freq_col(l):
        return PAR[0:64, 1, 94 + l:95 + l]

    P.MS('pool', WG[:], 0.0, [], [rSW])
    for l in range(2):
        for d in range(2):
            P.dma('pool', WG[d * 16:(d + 1) * 16, l, d, :], W['gla_wgk'][l, d], wr=[rSW])
        P.dma('pool', BGK[0:1, l, :, :], W['gla_bgk'][l:l + 1], wr=[rSW])
        P.dma('sp', HW1[:, l, :], W['hy_w1'][l], wr=[rSW])
        P.dma('sp', HW2[:, l, :], W['hy_w2'][l], wr=[rSW])
        P.TT('dve', HFB[:, l, 0:1], PAR[0:64, 1, 94 + l:95 + l], PAR[0:64, 1, 96 + l:97 + l], ALU.mult, [rPAR], [rSW])
        P.TT('dve', HFB[:, l, 1:2], PAR[0:64, 1, 94 + l:95 + l], PAR[0:64, 1, 98 + l:99 + l], ALU.mult, [rPAR], [rSW])

    P.ACT(SCB[:].rearrange("p k v -> p v k"), PAR[:, 1, 64:80].rearrange("p (v k) -> p v k", v=2), AF.Silu, [rPAR], [rSCB])
    rMODw = [[P.res("MOD%d_%d" % (l, w)) for w in range(6)] for l in range(2)]
    rGGw = [[P.res("GG%d_%d" % (l, w)) for w in range(4)] for l in range(2)]
    _mgh = [None]

    def mods_gen():
      for l in range(CFG["layers"]):
        for blk in range(6):
            wb, rw = wbuf()
            wv = wb[:, 0:8192].rearrange("p (k n) -> p k n", k=8)
            P.dma('pool', wv, W['w_mod'][l][:, blk * 1024:(blk + 1) * 1024].rearrange("(k p) n -> p k n", p=128), wr=[rw])
            mods_pending[0] = 0
            yield
            mods_pending[0] = None
            b, rb = P.bank()
            for j in range(8):
                P.MMG(b[:, 2 * j:2 * j + 2], [(wv[:, kc, j * 128:(j + 1) * 128], SCB[:, kc, :]) for kc in range(8)],
                      [rw, rSCB], [rb])
            P.TT('dve', MOD[:, l, blk * 8:(blk + 1) * 8, :], b[:, 0:16].rearrange("p (j v) -> p j v", v=2),
                 bmod_col(l, blk * 8, blk * 8 + 8).unsqueeze(2).to_broadcast([128, 8, 2]), ALU.add, [rb, rPAR], [rMODw[l][blk]])
            if blk == 1:
                P.STT('dve', GG[:, l, 0], MOD[:, l, 8:16, :], 1.0, gv_col(l, 0).unsqueeze(2).to_broadcast([128, 8, 2]),
                      ALU.add, ALU.mult, [rMODw[l][1], rPAR], [rGGw[l][0]])
            if blk == 4:
                P.STT('dve', GG[:, l, 1], MOD[:, l, 32:40, :], 1.0, gv_col(l, 2).unsqueeze(2).to_broadcast([128, 8, 2]),
                      ALU.add, ALU.mult, [rMODw[l][4], rPAR], [rGGw[l][1]])
            if blk == 2:
                P.TT('dve', GG[:, l, 2], MOD[:, l, 16:24, :], gv_col(l, 1).unsqueeze(2).to_broadcast([128, 8, 2]),
                     ALU.mult, [rMODw[l][2], rPAR], [rGGw[l][2]])
            if blk == 5:
                P.TT('dve', GG[:, l, 3], MOD[:, l, 40:48, :], gv_col(l, 3).unsqueeze(2).to_broadcast([128, 8, 2]),
                     ALU.mult, [rMODw[l][5], rPAR], [rGGw[l][3]])
            mods_done[0] += 1
            yield

    mods_done = [0]
    _mg = mods_gen()
    _mgh[0] = _mg

    def mods_until(l, blk):
        while mods_done[0] < l * 6 + blk + 1:
            try:
                next(_mg)
            except StopIteration:
                return

    def mods_step(n=1):
        for _ in range(n):
            try:
                next(_mg)
            except StopIteration:
                return

    mods_want = [False]

    def mods_dma():
        mods_want[0] = True

    def after_load():
        if mods_want[0]:
            mods_want[0] = False
            if mods_pending[0] is None:
                mods_step(1)

    def mods_mm():
        if mods_pending[0] is not None:
            mods_step(1)


    def rstd_of(src, nch, rd, lhs_ones, n=512):
        b, rb = P.bank()
        if nch == 8:
            for i in range(4):
                P.ACT(SQ[:, 2 * i:2 * i + 2, 0:n], src[:, 2 * i:2 * i + 2, :], AF.Square, rd, [rSQc[i]])
                for fc in (2 * i, 2 * i + 1):
                    P.MM(b[:, 0:n], lhs_ones, SQ[:, fc, 0:n], fc == 0, fc == 7, [rSQc[i], rC], [rb])
        else:
            P.ACT(SQ[:, 0:nch, 0:n], src, AF.Square, rd, rSQc)
            P.MMG(b[:, 0:n], [(lhs_ones, SQ[:, fc, 0:n]) for fc in range(nch)], rSQc + [rC], [rb])
        P.ACT(RS[:, 0:n], b[:, 0:n], AF.Ln, [rb, rcst], [rRS], bias=epsc, scale=1.0)
        P.ACT(RS[:, 0:n], RS[:, 0:n], AF.Exp, [rRS], [rRS], scale=-0.5)

    def norm_mod(l, blk, gi, sh0, v, sl=None):
        mods_until(l, 1 if gi == 0 else 4)
        if sl is None:
            sl = slice(blk * 512, (blk + 1) * 512)
        n = sl.stop - sl.start
        rstd_of(XT[:, :, sl], 8, [rXT[blk]], onesd[:], n)
        for fc in range(8):
            P.STT('dve', TMPF[:, fc % 2, 0:n], XT[:, fc, sl], GG[:, l, gi, fc, v:v + 1], RS[:, 0:n], ALU.mult, ALU.mult,
                  [rXT[blk], rGGw[l][gi], rRS], [rTMPF[fc % 2]])
            P.ACT(HT[:, fc, sl], TMPF[:, fc % 2, 0:n], AF.Identity, [rTMPF[fc % 2], rMODw[l][sh0 // 8]], [rHT[blk][fc]],
                  bias=MOD[:, l, sh0 + fc, v:v + 1], scale=1.0)

    def fproj(wv, rw, c0, m, blk, sl=None):
        b, rb = P.bank()
        if sl is None:
            sl = slice(blk * 512, (blk + 1) * 512)
        n = sl.stop - sl.start
        P.MMG(b[0:m, 0:n], [(wv[:, kc, c0:c0 + m], HT[:, kc, sl]) for kc in range(8)], [rw] + rHT[blk], [rb])
        return b, rb

    def tproj(wv, rw, c0, n, tt):
        b, rb = P.bank()
        P.MMG(b[:, 0:n], [(HT[:, kc, tt * 128:(tt + 1) * 128], wv[:, kc, c0:c0 + n]) for kc in range(8)],
              [rw] + rHT[tt // 4], [rb])
        return b, rb

    def load_w_in(l, c0, n):
        wb, rw = wbuf()
        wv = wb[:, 0:8 * n].rearrange("p (k n) -> p k n", k=8)
        P.dma('pool', wv, W['w_in'][l][:, c0:c0 + n].rearrange("(k p) n -> p k n", p=128), wr=[rw])
        after_load()
        return wv, rw

    def conv3(dst, src, w0, w1, w2, row_w, rd, wr):
        P.TS('pool', dst, src, w1, None, ALU.mult, None, rd, wr)
        d3 = dst.rearrange("p (r c) -> p r c", c=row_w)
        s3 = src.rearrange("p (r c) -> p r c", c=row_w)
        P.STT('dve', d3[:, :, 1:row_w], s3[:, :, 0:row_w - 1], w0, d3[:, :, 1:row_w], ALU.mult, ALU.add, rd + wr, wr)
        P.STT('pool', d3[:, :, 0:row_w - 1], s3[:, :, 1:row_w], w2, d3[:, :, 0:row_w - 1], ALU.mult, ALU.add, rd + wr, wr)

    def run_group(gname):
        is_p = gname == "P"
        x_d = xp_d if is_p else xs_d
        y_d = yp_d if is_p else ys_d
        v = 0 if is_p else 1
        row_w = 256 if is_p else 64
        L = 256 if is_p else 1024
        ntl = L // 128
        seqs = [(s * 2, 2) for s in range(4)] if is_p else [(0, 8)]

        P.tag = "LOAD" + gname + (str(l) if "l" in dir() else "")
        aTT.next_phase()
        rXS = [aTT.res("XS0"), aTT.res("XS1")]
        for tt in range(8):
            k = tt % 2
            P.dma('sp', XS[:, k, :], x_d[tt * 128:(tt + 1) * 128, :], wr=[rXS[k]])
            for half in range(2):
                b, rb = P.bank()
                for j in range(4):
                    P.TR(b[:, j * 128:(j + 1) * 128], XS[:, k, (half * 4 + j) * 128:(half * 4 + j + 1) * 128], ident[:],
                         [rXS[k], rC], [rb], inc=(j == 3))
                P.EV(XT[:, half * 4:half * 4 + 4, tt * 128:(tt + 1) * 128], b[:, :].rearrange("p (j t) -> p j t", j=4),
                     [rb], [rXT[tt // 4]])
        mods_step(4)

        for l in range(CFG["layers"]):
            if l > 0:
                pass
            P.tag = "N1" + gname + (str(l) if "l" in dir() else "")
            for blk in range(2):
                norm_mod(l, blk, 0, 0, v)

            mods_dma()
            aBIG.next_phase()
            MIN = BIG[:, 0:4096].bitcast(BF16).rearrange("p (c t) -> p c t", c=8)
            rMIN = [aBIG.res("MIN%d" % i) for i in range(8)]
            PTMP = BIG[:, 4096:10240].rearrange("p (c t) -> p c t", c=6)
            rPT = [aBIG.res("PT%d" % i) for i in range(6)]
            for ci in range(8):
                ch = "AABBCCDD"[ci]
                if ch not in CFG["mixers"]:
                    P.MS('pool', MIN[:, ci, :], 0.0, [], [rMIN[ci]])

            P.tag = "A" + gname + (str(l) if "l" in dir() else "")
            if "A" in CFG["mixers"]:
                aAR2.next_phase()
                CG = AR2[:, 0:1024]
                U = AR2[:, 1024:2048]
                ACC = AR2[:, 2048:3072]
                BG = AR2[:, 3072:4096]
                rCG, rU, rACC, rBG = (aAR2.res() for _ in range(4))
                wv, rw = load_w_in(l, 0, 768)
                for ch in range(2):
                    for blk in range(2):
                        sl = slice(blk * 512, (blk + 1) * 512)
                        b, rb = fproj(wv, rw, 256 + ch * 128, 128, blk)
                        P.CP('act', CG[:, sl], b[:, :], [rb], [rCG])
                        b2, rb2 = fproj(wv, rw, 512 + ch * 128, 128, blk)
                        P.TT('dve', U[:, sl], CG[:, sl], b2[:, :], ALU.mult, [rCG, rb2], [rU])
                        b3, rb3 = fproj(wv, rw, ch * 128, 128, blk)
                        P.CP('act', BG[:, sl], b3[:, :], [rb3], [rBG])
                    mods_mm()
                    conv3(ACC, U, conva_col(l, 0, ch), conva_col(l, 1, ch), conva_col(l, 2, ch), row_w, [rU, rPAR], [rACC])
                    P.TT('dve', MIN[:, ch, :], ACC, BG, ALU.mult, [rACC, rBG], [rMIN[ch]])

            mods_mm()
            mods_dma()
            P.tag = "B" + gname + (str(l) if "l" in dir() else "")
            if "B" in CFG["mixers"]:
                aAR2.next_phase()
                QT = AR2[:, 0:1024]
                KT = AR2[:, 1024:2048]
                GKT = AR2[:, 2048:2560].bitcast(BF16)
                K_T = AR2[:, 2560:3584].rearrange("p (t k) -> p t k", t=8)
                V_T = AR2[:, 3584:4608].bitcast(BF16).rearrange("p (t k) -> p t k", t=8)
                OT = AR2[:, 4608:6656].rearrange("p (h t) -> p h t", h=2)
                SG = AR2[:, 6656:7680].bitcast(BF16).rearrange("p (h t) -> p h t", h=2)
                rQT, rKT, rGKT, rK_T, rV_T, rSG = (aAR2.res() for _ in range(6))
                rOT = [[aAR2.res() for _ in range(8)] for _ in range(2)]
                wv, rw = load_w_in(l, 768, 800)
                aTT.next_phase()
                gslots = [[aTT.res() for _ in range(6)] + [aTT.res(), aTT.res(), aTT.res(), rTMPF[sl_]] * 1 + [aTT.res(), aTT.res(), aTT.res(), rTMPF[sl_]] for sl_ in range(2)]
                for blk in range(2):
                    sl = slice(blk * 512, (blk + 1) * 512)
                    b, rb = fproj(wv, rw, 0, 128, blk)
                    P.CP('act', QT[:, sl], b[:, :], [rb], [rQT])
                    b, rb = fproj(wv, rw, 128, 128, blk)
                    P.CP('dve', KT[:, sl], b[:, :], [rb], [rKT])
                    b, rb = fproj(wv, rw, 768, 32, blk)
                    P.CP('act', GKT[0:32, sl], b[0:32, :], [rb], [rGKT])
                for tt in range(8):
                    b, rb = tproj(wv, rw, 128, 384, tt)
                    P.CP('act', K_T[:, tt, :], b[:, 0:128], [rb], [rK_T])
                    P.CP('dve', V_T[:, tt, :], b[:, 128:384], [rb], [rV_T])
                for hp in range(2):
                    P.MS('dve', OT[:, hp, :], 0.0, [], rOT[hp])
                P.tag = "Bs" + gname + str(l)

                def gla_chain(d, si, t0, nt, slot):
                    sidx = d * 2 + (si % 2)
                    S = SST[:, sidx, :]
                    rS = rSST[sidx]
                    P.MS('dve', S, 0.0, [], [rS])
                    if not is_p:
                        for h in range(4):
                            P.dma('sp', SST[32 * h:32 * h + 32, sidx, 64 * h:64 * h + 64], s0_d[l, d, h], rd=[], wr=[rS])
                    P.CP('act', SBF[:, d, :], S, [rS], [rSBF[d]])
                    yield
                    order = list(range(t0, t0 + nt)) if d == 0 else list(range(t0 + nt - 1, t0 - 1, -1))
                    o_ = slot * 1800
                    LP = TTB[:, o_ + 0:o_ + 128]
                    EBD = TTB[:, o_ + 128:o_ + 384]
                    EB = EBD[:, 0:128]
                    ED = EBD[:, 128:256]
                    ENB = TTB[:, o_ + 384:o_ + 512]
                    QE = TTB[:, o_ + 512:o_ + 640]
                    KTL = TTB[:, o_ + 640:o_ + 704].bitcast(BF16)
                    KH = TTB[:, o_ + 704:o_ + 768].bitcast(BF16)
                    rLP, rEBD, rENB, rQE, rKTL, rKH = gslots[slot][0:6]
                    Q4s, AMs, USs, ACs, rQ4s, rAMs, rUSs, rACs = [], [], [], [], [], [], [], []
                    for ps in range(2):
                        q_ = o_ + 768 + ps * 516
                        Q4s.append(TTB[:, q_:q_ + 256].bitcast(BF16))
                        AMs.append(TTB[:, q_ + 256:q_ + 512].bitcast(BF16))
                        ACs.append(TTB[:, q_ + 512:q_ + 513])
                        USs.append(TMPF[:, slot, ps * 256:(ps + 1) * 256])
                        rQ4s.append(gslots[slot][6 + ps * 4])
                        rAMs.append(gslots[slot][7 + ps * 4])
                        rACs.append(gslots[slot][8 + ps * 4])
                        rUSs.append(gslots[slot][9 + ps * 4])

                    def prep(tt, ps):
                        tsl = slice(tt * 128, (tt + 1) * 128)
                        Q4, AM, US, AC = Q4s[ps], AMs[ps], USs[ps], ACs[ps]
                        rQ4, rAM, rUS, rAC = rQ4s[ps], rAMs[ps], rUSs[ps], rACs[ps]
                        b, rb = P.bank()
                        P.MM(b[:, 0:128], GKT[0:32, tsl], WG[0:32, l, d, :], True, False, [rGKT, rSW], [rb])
                        P.MM(b[:, 0:128], onesr[0:1, :], BGK[0:1, l, d, :], False, True, [rC, rSW], [rb])
                        yield
                        P.ACT(LP, b[:, 0:128], AF.Exp, [rb], [rLP], scale=-1.0)
                        P.ACT(LP, LP, AF.Ln, [rLP], [rLP], bias=1.0, scale=1.0)
                        yield
                        b1, rb1 = P.bank()
                        P.MM(b1[:, 0:128], LP, tri[:, d * 2, :], True, True, [rLP, rC], [rb1])
                        P.MM(b1[:, 128:256], tri[:, d * 2 + 1, :], LP, True, True, [rLP, rC], [rb1])
                        yield
                        P.ACT(EBD, b1[:, 0:256], AF.Exp, [rb1], [rEBD])
                        P.ACT(ENB, b1[:, 0:128], AF.Exp, [rb1], [rENB], scale=-1.0)
                        yield
                        P.STT('dve', QE, QT[:, tsl], 32.0 ** -0.5, EB, ALU.mult, ALU.mult, [rQT, rEBD], [rQE])
                        P.TT('gp', Q4.rearrange("p (h i) -> p h i", h=4), QE.unsqueeze(1).to_broadcast([128, 4, 128]),
                             hm4[:], ALU.mult, [rQE, rC], [rQ4])
                        P.TT('dve', KTL, KT[:, tsl], ENB, ALU.mult, [rKT, rENB], [rKTL])
                        P.TT('gp', KH, K_T[:, tt, :], ED, ALU.mult, [rK_T, rEBD], [rKH])
                        P.CP('act', AC, EB[:, 127:128] if d == 0 else EB[:, 0:1], [rEBD], [rAC])
                        yield
                        b2, rb2 = P.bank()
                        P.MM(b2[:, :], KTL, Q4, True, True, [rKTL, rQ4], [rb2])
                        b3, rb3 = P.bank()
                        P.MM(b3[:, 0:256], KH, V_T[:, tt, :], True, True, [rKH, rV_T], [rb3])
                        yield
                        P.TT('dve', AM.rearrange("p (h i) -> p h i", h=4), b2[:, :].rearrange("p (h i) -> p h i", h=4),
                             cmask[:, d, :].unsqueeze(1).to_broadcast([128, 4, 128]), ALU.mult, [rb2, rC], [rAM])
                        P.CP('act', US, b3[:, 0:256], [rb3], [rUS])
                        yield

                    def rec(tt, ps):
                        tsl = slice(tt * 128, (tt + 1) * 128)
                        Q4, AM, US, AC = Q4s[ps], AMs[ps], USs[ps], ACs[ps]
                        rQ4, rAM, rUS, rAC = rQ4s[ps], rAMs[ps], rUSs[ps], rACs[ps]
                        b4, rb4 = P.bank()
                        for hp in range(2):
                            P.MM(b4[:, hp * 256:(hp + 1) * 256], V_T[:, tt, hp * 128:(hp + 1) * 128],
                                 AM[:, hp * 256:(hp + 1) * 256], True, False, [rV_T, rAM], [rb4])
                            P.MM(b4[:, hp * 256:(hp + 1) * 256], SBF[:, d, hp * 128:(hp + 1) * 128],
                                 Q4[:, hp * 256:(hp + 1) * 256], False, True, [rSBF[d], rQ4], [rb4])
                        yield
                        P.STT('dve', S, S, AC, US, ALU.mult, ALU.add, [rS, rAC, rUS], [rS])
                        P.CP('act', SBF[:, d, :], S, [rS], [rSBF[d]])
                        yield
                        for hl in range(2):
                            rows = slice(hl * 64, (hl + 1) * 64)
                            src = b4[rows, hl * 128:hl * 128 + 384].rearrange("p (h c) -> p h c", c=128)[:, 0:3:2, :]
                            P.TT('dve', OT[rows, :, tsl], OT[rows, :, tsl], src, ALU.add,
                                 [rb4, rOT[0][tt], rOT[1][tt]], [rOT[0][tt], rOT[1][tt]])
                            yield

                    yield from prep(order[0], 0)
                    for k in range(nt):
                        subs = [rec(order[k], k % 2)]
                        if k + 1 < nt:
                            subs.append(prep(order[k + 1], (k + 1) % 2))
                        while subs:
                            for g in list(subs):
                                try:
                                    next(g)
                                except StopIteration:
                                    subs.remove(g)
                            yield
                    if is_p:
                        k = (d * 4 + si) % 2
                        for h in range(4):
                            P.CP('act', SCMP[32 * h:32 * h + 32, k, :], SST[32 * h:32 * h + 32, sidx, 64 * h:64 * h + 64],
                                 [rS], [rSCMP[k]])
                        P.dma('sp', ns_d[si, l, d], SCMP[:, k, :], rd=[rSCMP[k]], is_out=True)
                    yield

                for si, (t0, nt) in enumerate(seqs):
                    mods_mm()
                    zip_run([gla_chain(0, si, t0, nt, 0), gla_chain(1, si, t0, nt, 1)])
                P.tag = "Bg" + gname + str(l)
                for hp in range(2):
                    for blk in range(2):
                        sl = slice(blk * 512, (blk + 1) * 512)
                        rstd_of(OT[:, hp:hp + 1, sl], 1, rOT[hp][blk * 4:blk * 4 + 4], bd64[:])
                        P.STT('dve', OT[:, hp, sl], OT[:, hp, sl], gn_col(l), RS[:], ALU.mult, ALU.mult,
                              rOT[hp][blk * 4:blk * 4 + 4] + [rPAR, rRS], rOT[hp][blk * 4:blk * 4 + 4])
                for hp in range(2):
                    for blk in range(2):
                        sl = slice(blk * 512, (blk + 1) * 512)
                        b, rb = fproj(wv, rw, 512 + hp * 128, 128, blk)
                        P.ACT(SG[:, hp, sl], b[:, :], AF.Silu, [rb], [rSG])
                        P.TT('pool', MIN[:, 2 + hp, sl], OT[:, hp, sl], SG[:, hp, sl], ALU.mult,
                             rOT[hp][blk * 4:blk * 4 + 4] + [rSG], [rMIN[2 + hp]])

            mods_mm()
            mods_dma()
            P.tag = "C" + gname + (str(l) if "l" in dir() else "")
            if "C" in CFG["mixers"]:
                aAR2.next_phase()
                UG = AR2[:, 0:2048].rearrange("p (g t) -> p g t", g=2)
                VVN = AR2[:, 2048:3072].bitcast(BF16).rearrange("p (t c) -> p t c", t=8)
                WSI = AR2[:, 3072:3584].rearrange("p (g j) -> p g j", g=4)
                WST = AR2[:, 3584:3840].bitcast(BF16)
                BST = AR2[:, 3840:4096].rearrange("p (g i) -> p g i", g=2)
                LNG = AR2[:, 4096:4352]
                rUG, rWSI, rWST, rBST, rLNG = (aAR2.res() for _ in range(5))
                rVVN = [aAR2.res() for _ in range(8)]
                wv, rw = load_w_in(l, 1568, 512)
                P.dma('sp', WSI, W['mlp_ws'][l].rearrange("g i j -> i g j"), wr=[rWSI])
                b, rb = P.bank()
                for g in range(4):
                    P.TR(b[:, g * 128:(g + 1) * 128], WSI[:, g, :], ident[:], [rWSI, rC], [rb], inc=(g == 3))
                P.CP('act', WST, b[:, :], [rb], [rWST])
                for gp in range(2):
                    for hl in range(2):
                        P.dma('sp', BST[hl * 64:(hl + 1) * 64, gp, :], W['mlp_bs'][l, 2 * gp + hl].partition_broadcast(64), wr=[rBST])
                P.dma('sp', LNG, W['mlp_ln'][l].partition_broadcast(128), wr=[rLNG])
                for gp in range(2):
                    for blk in range(2):
                        sl = slice(blk * 512, (blk + 1) * 512)
                        b, rb = fproj(wv, rw, gp * 128, 128, blk)
                        P.ACT(UG[:, gp, sl], b[:, :], AF.Gelu_apprx_tanh, [rb], [rUG])
                aTT.next_phase()
                if CFG.get("c_batched", True):
                    rGVa, rSQa = aTT.res(), aTT.res()
                    GVa = TTB[:, 0:2048].rearrange("p (t c) -> p t c", t=8)
                    SQa = TTB[:, 2048:4096].rearrange("p (t c) -> p t c", t=8)
                    STa = RS[:, 0:32].rearrange("p (a t) -> p a t", a=4)
                    mods_mm()
                    for tt in range(8):
                        b, rb = tproj(wv, rw, 256, 256, tt)
                        P.ACT(GVa[:, tt, :], b[:, 0:256], AF.Gelu_apprx_tanh, [rb], [rGVa])
                    P.op('dve', lambda e: e.reduce_sum(out=STa[:, 0, :], in_=GVa, axis=AX.X), [rGVa], [rRS])
                    P.TS('dve', STa[:, 1, :], STa[:, 0, :], -1.0 / 256, None, ALU.mult, None, [rRS], [rRS])
                    P.TT('dve', GVa, GVa, STa[:, 1, :].unsqueeze(2).to_broadcast([128, 8, 256]), ALU.add, [rGVa, rRS], [rGVa])
                    P.TT('dve', SQa, GVa, GVa, ALU.mult, [rGVa], [rSQa])
                    P.op('dve', lambda e: e.reduce_sum(out=STa[:, 2, :], in_=SQa, axis=AX.X), [rSQa], [rRS])
                    P.ACT(STa[:, 3, :], STa[:, 2, :], AF.Ln, [rRS, rcst], [rRS], bias=epsc, scale=1.0 / 256)
                    P.ACT(STa[:, 3, :], STa[:, 3, :], AF.Exp, [rRS], [rRS], scale=-0.5)
                    P.TT('dve', SQa, GVa, STa[:, 3, :].unsqueeze(2).to_broadcast([128, 8, 256]), ALU.mult, [rGVa, rRS], [rSQa])
                    P.TT('dve', VVN, SQa, LNG.unsqueeze(1).to_broadcast([128, 8, 256]), ALU.mult, [rSQa, rLNG], rVVN)
                    T1a = TTB[:, 0:2048].rearrange("p (t g i) -> p t g i", t=8, g=2)
                    for tt in range(8):
                        for gp in range(2):
                            b, rb = P.bank()
                            P.MM(b[:, 0:256], VVN[:, tt, gp * 128:(gp + 1) * 128], WST[:, gp * 256:(gp + 1) * 256], True, True,
                                 rVVN + [rWST], [rb])
                            for hl in range(2):
                                rows = slice(hl * 64, (hl + 1) * 64)
                                P.TT('dve', T1a[rows, tt, gp, :], b[rows, hl * 128:(hl + 1) * 128], BST[rows, gp, :], ALU.add,
                                     [rb, rBST, rGVa], [rGVa])
                    for gp in range(2):
                        for hl in range(2):
                            rows = slice(hl * 64, (hl + 1) * 64)
                            P.TT('dve', MIN[rows, 4 + gp, :].rearrange("p (t i) -> p t i", t=8), T1a[rows, :, gp, :],
                                 UG[rows, gp, :].rearrange("p (t i) -> p t i", t=8), ALU.mult, [rGVa, rUG], [rMIN[4 + gp]])
                else:
                  cslots = [[aTT.res() for _ in range(5)] for _ in range(4)]

                  def mlp_tile(tt, slot):
                    tsl = slice(tt * 128, (tt + 1) * 128)
                    o_ = slot * 1024
                    GV = TTB[:, o_ + 0:o_ + 256]
                    CEN = TTB[:, o_ + 256:o_ + 512]
                    SQV = TTB[:, o_ + 512:o_ + 768]
                    ST = TTB[:, o_ + 768:o_ + 772]
                    T1 = TTB[:, o_ + 896:o_ + 1024]
                    rGV, rCEN, rSQV, rST, rT1 = cslots[slot]
                    b, rb = tproj(wv, rw, 256, 256, tt)
                    yield
                    P.ACT(GV, b[:, 0:256], AF.Gelu_apprx_tanh, [rb], [rGV])
                    yield
                    P.op('dve', lambda e, o=ST[:, 0:1], i=GV: e.reduce_sum(out=o, in_=i, axis=AX.X), [rGV], [rST])
                    yield
                    P.TS('dve', ST[:, 1:2], ST[:, 0:1], -1.0 / 256, None, ALU.mult, None, [rST], [rST])
                    yield
                    P.TS('dve', CEN, GV, ST[:, 1:2], None, ALU.add, None, [rGV, rST], [rCEN])
                    yield
                    P.TT('dve', SQV, CEN, CEN, ALU.mult, [rCEN], [rSQV])
                    yield
                    P.op('dve', lambda e, o=ST[:, 2:3], i=SQV: e.reduce_sum(out=o, in_=i, axis=AX.X), [rSQV], [rST])
                    yield
                    P.ACT(ST[:, 3:4], ST[:, 2:3], AF.Ln, [rST, rcst], [rST], bias=epsc, scale=1.0 / 256)
                    yield
                    P.ACT(ST[:, 3:4], ST[:, 3:4], AF.Exp, [rST], [rST], scale=-0.5)
                    yield
                    P.STT('dve', VVN[:, tt, :], CEN, ST[:, 3:4], LNG, ALU.mult, ALU.mult, [rCEN, rST, rLNG], [rVVN[tt]])
                    yield
                    for gp in range(2):
                        b, rb = P.bank()
                        P.MM(b[:, 0:256], VVN[:, tt, gp * 128:(gp + 1) * 128], WST[:, gp * 256:(gp + 1) * 256], True, True,
                             [rVVN[tt], rWST], [rb])
                        yield
                        for hl in range(2):
                            rows = slice(hl * 64, (hl + 1) * 64)
                            P.TT('dve', T1[rows, :], b[rows, hl * 128:(hl + 1) * 128], BST[rows, gp, :], ALU.add,
                                 [rb, rBST], [rT1])
                            yield
                            P.TT('gp', MIN[rows, 4 + gp, tsl], T1[rows, :], UG[rows, gp, tsl], ALU.mult,
                                 [rT1, rUG], [rMIN[4 + gp]])
                            yield

                  for grp in ((0, 1, 2, 3), (4, 5, 6, 7)):
                    mods_mm()
                    zip_run([mlp_tile(tt, i) for i, tt in enumerate(grp)])

            mods_mm()
            mods_dma()
            P.tag = "D" + gname + (str(l) if "l" in dir() else "")
            if "D" in CFG["mixers"]:
                nL = 2 * L
                aTT.next_phase()
                HNQ = TMPF[0:1, 0, :]
                PRAW = TTB[:, 0:1024].rearrange("p (k t) -> p k t", k=2)
                rPRAW = [aTT.res(), aTT.res()]
                rHNQ = rTMPF[0]
                rZR, rZS = aTT.res(), aTT.res()
                rFS1a, rFS1b = aTT.res(), aTT.res()
                rFS2a, rFS2b = aTT.res(), aTT.res()
                wv, rw = load_w_in(l, 2080, 768)
                for ci in range(6):
                    for blk in range(2):
                        sl = slice(blk * 512, (blk + 1) * 512)
                        k = (ci * 2 + blk) % 2
                        b, rb = fproj(wv, rw, ci * 128, 128, blk)
                        P.CP('act', PRAW[:, k, :], b[:, :], [rb], [rPRAW[k]])
                        conv3(PTMP[:, ci, sl], PRAW[:, k, :], hyconv_col(l, 0, ci), hyconv_col(l, 1, ci), hyconv_col(l, 2, ci),
                              row_w, [rPRAW[k], rPAR], [rPT[ci]])
                if is_p:
                    CSv = CS256
                    rCS = [rCS256]
                    wcol = wcol256
                    negt = negt256
                else:
                    P.dma('pool', WBR[:].rearrange("p (k n) -> p k n", k=8),
                          C["cs1024"].rearrange("(k p) n -> p k n", p=128), wr=[rWB[0], rWB[1]])
                    CSv = WBR[:].rearrange("p (k n) -> p k n", k=8)
                    rCS = [rWB[0], rWB[1]]
                    wcol = wcol1024
                    negt = negt1024
                P.tag = "Df" + gname + str(l)
                mods_mm()
                aAR2.next_phase()
                HS = AR2[:, 0:2048].bitcast(BF16).rearrange("p (t c) -> p t c", t=8)
                HD = AR2[:, 2048:4096].bitcast(BF16).rearrange("p (t c) -> p t c", t=8)
                HR = AR2[:, 4096:6144].bitcast(BF16).rearrange("p (t c) -> p t c", t=8)
                HI = AR2[:, 6144:8192].bitcast(BF16).rearrange("p (t c) -> p t c", t=8)
                rHS, rHD = aAR2.res(), aAR2.res()
                H1 = AR2[0:64, 4096:5120]
                H2 = AR2[0:64, 5120:6144]
                DEC = AR2[:, 6144:7168]
                HF = AR2[:, 7168:8192]
                rH1, rH2, rDEC, rHF = (aAR2.res() for _ in range(4))
                ARG = TTB[0:64, 0:512]
                M1 = TTB[0:64, 512:1024]
                rARG, rM1 = rPRAW[0], rPRAW[1]
                fcol = freq_col(l)

                def sinlayer(wT, src, rsrc, fb, dst, rdst):
                    n = min(512, L)
                    for c in range(L // n):
                        b, rb = P.bank()
                        P.MM(b[0:64, 0:n], wT, src[:, c * n:(c + 1) * n], True, True, [rSW, rsrc], [rb])
                        P.TS('dve', ARG[:, 0:n], b[0:64, 0:n], fcol, fb, ALU.mult, ALU.add, [rb, rPAR, rSW], [rARG])
                        for _ in range(2):
                            P.TS('dve', M1[:, 0:n], ARG[:, 0:n], -PI, 2 * PI, ALU.is_lt, ALU.mult, [rARG], [rM1])
                            P.TT('dve', ARG[:, 0:n], ARG[:, 0:n], M1[:, 0:n], ALU.add, [rARG, rM1], [rARG])
                            P.TS('dve', M1[:, 0:n], ARG[:, 0:n], PI, -2 * PI, ALU.is_gt, ALU.mult, [rARG], [rM1])
                            P.TT('dve', ARG[:, 0:n], ARG[:, 0:n], M1[:, 0:n], ALU.add, [rARG, rM1], [rARG])
                        P.ACT(dst[:, c * n:(c + 1) * n], ARG[:, 0:n], AF.Sin, [rARG], [rdst])

                FT = AR2[0:17, 5120:5120 + L]
                P.dma('sp', FT, C["feat%d" % L], wr=[rH2])
                EDB = TTB[:, 1024:2048]
                rEDB = [rZR, rZS]
                P.dma('sp', EDB, W['hy_log_decay'][l].rearrange("d o c -> (d o c)").partition_broadcast(128), wr=rEDB)
                P.ACT(EDB, EDB, AF.Exp, rEDB, rEDB)
                sinlayer(HW1[:, l, :], FT, rH2, HFB[:, l, 0:1], H1, rH1)
                sinlayer(HW2[:, l, :], H1[:, 0:L], rH1, HFB[:, l, 1:2], H2, rH2)
                HW3 = TTB[0:64, 0:1024]
                rHW3 = [rARG, rM1]
                P.dma('sp', HW3, W['hy_w3'][l], wr=rHW3)
                for pt in range(ntl):
                    psl = slice(pt * 128, (pt + 1) * 128)
                    P.ACT(DEC, EDB, AF.Exp, rEDB + [rC], [rDEC], scale=negt[:, pt:pt + 1])
                    for hf in range(2):
                        csl = slice(hf * 512, (hf + 1) * 512)
                        b, rb = P.bank()
                        P.MM(b[:, :], H2[:, psl], HW3[:, csl], True, True, [rH2] + rHW3, [rb])
                        P.TT('dve', HF[:, csl], b[:, :], DEC[:, csl], ALU.mult, [rb, rDEC], [rHF])
                    if pt == 0:
                        P.MS('pool', HF[0:1, 512:1024], 0.0, [], [rHF])
                        P.dma('sp', DEC[0:1, 0:512], W['hy_bias'][l:l + 1].rearrange("a o c -> a (o c)"), wr=[rDEC])
                        P.TT('pool', HF[0:1, 0:512], HF[0:1, 0:512], DEC[0:1, 0:512], ALU.add, [rHF, rDEC], [rHF])
                    P.TT('dve', HS[:, pt, :], HF[:, 0:512], HF[:, 512:1024], ALU.add, [rHF], [rHS])
                    P.TT('pool', HD[:, pt, :], HF[:, 512:1024], HF[:, 0:512], ALU.subtract, [rHF], [rHD])
                P.tag = "Dh" + gname + str(l)
                aAR2.next_phase()
                rHS2, rHD2, rHR, rHI = (aAR2.res() for _ in range(4))
                for fc in range(ntl):
                    b, rb = P.bank()
                    P.MMG(b[:, :], [(CSv[:, dc, fc * 128:(fc + 1) * 128], HS[:, dc, :]) for dc in range(ntl)],
                          rCS + [rHS, rHS2], [rb])
                    P.ACT(HR[:, fc, :], b[:, :], AF.Identity, [rb, rC], [rHR], scale=wcol[:, fc:fc + 1])
                    b, rb = P.bank()
                    P.MMG(b[:, :], [(CSv[:, dc, L + fc * 128:L + (fc + 1) * 128], HD[:, dc, :]) for dc in range(ntl)],
                          rCS + [rHD, rHD2], [rb])
                    P.ACT(HI[:, fc, :], b[:, :], AF.Identity, [rb, rC], [rHI], scale=wcol[:, fc:fc + 1])
                b, rb = P.bank()
                P.MMG(b[0:1, :], [(altc[:, 0:1], HS[:, dc, :]) for dc in range(ntl)], [rC, rHS, rHS2], [rb])
                P.TS('dve', HNQ, b[0:1, :], 1.0 / nL, None, ALU.mult, None, [rb], [rHNQ])
                P.tag = "Dl" + gname + str(l)
                aAR2.next_phase()
                rHR2, rHI2 = aAR2.res(), aAR2.res()
                ZT = AR2[:, 0:1024].bitcast(BF16).rearrange("p (t c) -> p t c", t=8)
                Y = AR2[:, 1024:3072].bitcast(BF16).rearrange("p (j c) -> p j c", j=16)
                rZT, rYN = aAR2.res(), aAR2.res()
                rYj = [aAR2.res() for _ in range(16)]
                YN = AR2[0:1, 3072:3584].bitcast(BF16).rearrange("p (s c) -> p s c", s=4)
                fslots = [(TTB[:, 0:1024], rPRAW[0], rPRAW[1]), (TTB[:, 1024:2048], rZR, rZS),
                          (TTB[:, 2048:3072], rFS1a, rFS1b), (TTB[:, 3072:4096], rFS2a, rFS2b)]
                nseq = len(seqs)
                for o in range(2):
                    osl = slice(o * 256, (o + 1) * 256)
                    for tt in range(8):
                        b, rb = P.bank()
                        for cc in range(2):
                            P.TR(b[:, cc * 128:(cc + 1) * 128], PTMP[:, 4 + cc, tt * 128:(tt + 1) * 128], ident[:],
                                 [rPT[4 + cc], rC], [rb], inc=(cc == 1))
                        P.EV(ZT[:, tt, :], b[:, 0:256], [rb], [rZT])

                    def fchain(si, fcx, slot, osl=osl):
                        t0, nt = seqs[si]
                        yb = si * 2 * nt
                        TS_, rTa, rTb = fslots[slot]
                        T1, T2, T3, T4 = (TS_[:, i * 256:(i + 1) * 256] for i in range(4))
                        b, rb = P.bank()
                        P.MMG(b[:, 0:256], [(CSv[:, tl, fcx * 128:(fcx + 1) * 128], ZT[:, t0 + tl, :]) for tl in range(nt)],
                              rCS + [rZT], [rb])
                        P.MMG(b[:, 256:512], [(CSv[:, tl, L + fcx * 128:L + (fcx + 1) * 128], ZT[:, t0 + tl, :]) for tl in range(nt)],
                              rCS + [rZT], [rb])
                        yield
                        P.TT('dve', T1, b[:, 0:256], HR[:, fcx, osl], ALU.mult, [rb, rHR, rHR2], [rTa])
                        P.TT('dve', T2, b[:, 256:512], HI[:, fcx, osl], ALU.mult, [rb, rHI, rHI2], [rTa])
                        P.TT('dve', T3, b[:, 256:512], HR[:, fcx, osl], ALU.mult, [rb, rHR, rHR2], [rTb])
                        P.TT('dve', T4, b[:, 0:256], HI[:, fcx, osl], ALU.mult, [rb, rHI, rHI2], [rTb])
                        yield
                        P.TT('gp', Y[:, yb + fcx, :], T1, T2, ALU.add, [rTa], [rYj[yb + fcx]])
                        P.TT('gp', Y[:, yb + nt + fcx, :], T3, T4, ALU.subtract, [rTb], [rYj[yb + nt + fcx]])
                        yield

                    chains = [(si, fcx) for si in range(nseq) for fcx in range(seqs[si][1])]
                    for c0 in range(0, len(chains), 4):
                        zip_run([fchain(si, fcx, i) for i, (si, fcx) in enumerate(chains[c0:c0 + 4])])
                    for si, (t0, nt) in enumerate(seqs):
                        b, rb = P.bank()
                        P.MMG(b[0:1, 0:256], [(altc[:, 0:1], ZT[:, t0 + tl, :]) for tl in range(nt)], [rC, rZT], [rb])
                        P.TT('dve', YN[0:1, si, :], b[0:1, 0:256], HNQ[0:1, osl], ALU.mult, [rb, rHNQ], [rYN])
                    for si, (t0, nt) in enumerate(seqs):
                        yb = si * 2 * nt
                        n = min(512, L)
                        for cc in range(2):
                            csl = slice(cc * 128, (cc + 1) * 128)
                            for tb in range(L // n):
                                b, rb = P.bank()
                                pairs = []
                                for j in range(2 * nt):
                                    off = (L if j >= nt else 0) + tb * n
                                    pairs.append((Y[:, yb + j, csl], CSv[:, j % nt, off:off + n]))
                                pairs.append((YN[0:1, si, csl], altr[0:1, tb * n:tb * n + n]))
                                P.MMG(b[:, 0:n], pairs, rCS + rYj[yb:yb + 2 * nt] + [rYN, rC], [rb])
                                tok = slice(t0 * 128 + tb * n, t0 * 128 + tb * n + n)
                                if o == 0:
                                    P.TT('dve', PTMP[:, 4 + cc, tok], PTMP[:, cc, tok], b[:, 0:n], ALU.mult,
                                         [rPT[cc], rb, rPT[4 + cc]], [rPT[4 + cc]])
                                else:
                                    P.TT('dve', MIN[:, 6 + cc, tok], PTMP[:, 2 + cc, tok], b[:, 0:n], ALU.mult,
                                         [rPT[2 + cc], rb], [rMIN[6 + cc]])

            mods_mm()
            mods_dma()
            P.tag = "WO" + gname + (str(l) if "l" in dir() else "")
            mods_until(l, 2)
            aAR2.next_phase()
            MT = AR2[:, 0:4096].rearrange("p (c t) -> p c t", c=8)
            rMT = aAR2.res()
            wb, rw = wbuf()
            wv = wb[:, 0:8192].rearrange("p (k n) -> p k n", k=8)
            P.dma('pool', wv, W['w_out'][l].rearrange("(k p) n -> p k n", p=128), wr=[rw])
            after_load()
            own = (not is_p) and (l == CFG["layers"] - 1) and CFG.get("own", True)
            if own:
                P.dma_fn('sp', lambda e: e.dma_start(out=MIN[:, :, 0:256], in_=MIN[:, :, bass.ds(dyn['c'], 256)]), rMIN, rMIN)
                P.dma_fn('sp', lambda e: e.dma_start(out=XT[:, :, 0:256], in_=XT[:, :, bass.ds(dyn['c'], 256)]), rXT, rXT)
                blks = [(0, slice(0, 256))]
            else:
                blks = [(0, slice(0, 512)), (1, slice(512, 1024))]
            for blk, sl in blks:
                n = sl.stop - sl.start
                for m in range(8):
                    b, rb = P.bank()
                    P.MMG(b[:, 0:n], [(wv[:, kc, m * 128:(m + 1) * 128], MIN[:, kc, sl]) for kc in range(8)], [rw] + rMIN, [rb])
                    P.EV(MT[:, m, 0:n], b[:, 0:n], [rb], [rMT])
                rstd_of(MT[:, :, 0:n], 8, [rMT], onesd[:], n)
                for fc in range(8):
                    P.TT('dve', TMPF[:, fc % 2, 0:n], MT[:, fc, 0:n], RS[:, 0:n], ALU.mult, [rMT, rRS], [rTMPF[fc % 2]])
                    P.STT('pool', XT[:, fc, sl], TMPF[:, fc % 2, 0:n], GG[:, l, 2, fc, v:v + 1], XT[:, fc, sl], ALU.mult, ALU.add,
                          [rTMPF[fc % 2], rGGw[l][2], rXT[blk]], [rXT[blk]])
                norm_mod(l, blk, 1, 24, v, sl)

            mods_mm()
            mods_dma()
            P.tag = "FFN" + gname + (str(l) if "l" in dir() else "")
            aBIG.next_phase()
            AT = BIG[:, :].bitcast(BF16).rearrange("p (c t) -> p c t", c=22)
            rAT = [[aBIG.res() for _ in range(22)] for _ in range(2)]
            aTT.next_phase()
            SL = TTB[:, 0:1024].rearrange("p (k t) -> p k t", k=2)
            rSL = [aTT.res(), aTT.res()]
            cnt = 0
            cbs = [(i * 512, 512) for i in range(5)] + [(2560, 256)]
            for cbi, (c0, cw) in enumerate(cbs):
                if cbi == 1:
                    mods_mm()
                wb, rw = wbuf()
                w1v = wb[:, 0:8 * cw].rearrange("p (k n) -> p k n", k=8)
                w3v = wb[:, 4096:4096 + 8 * cw].rearrange("p (k n) -> p k n", k=8)
                P.dma('pool', w1v, W['ffn_w1'][l][:, c0:c0 + cw].rearrange("(k p) n -> p k n", p=128), wr=[rw])
                P.dma('pool', w3v, W['ffn_w3'][l][:, c0:c0 + cw].rearrange("(k p) n -> p k n", p=128), wr=[rw])
                after_load()
                order = [(j, bs) for bs in blks for j in range(cw // 128)] if cbi == 0 else [(j, bs) for j in range(cw // 128) for bs in blks]
                for j, (blk, sl) in order:
                    n = sl.stop - sl.start
                    b1, rb1 = fproj(w1v, rw, j * 128, 128, blk, sl)
                    b3, rb3 = fproj(w3v, rw, j * 128, 128, blk, sl)
                    k = cnt % 2
                    cnt += 1
                    P.ACT(SL[:, k, 0:n], b1[:, 0:n], AF.Silu, [rb1], [rSL[k]])
                    P.TT('dve', AT[:, c0 // 128 + j, sl], SL[:, k, 0:n], b3[:, 0:n], ALU.mult, [rSL[k], rb3], [rAT[blk][c0 // 128 + j]])
            mods_until(l, 5)
            aAR2.next_phase()
            MT = AR2[:, 0:8192].rearrange("p (c t) -> p c t", c=8)
            rMT2 = [aAR2.res(), aAR2.res()]
            for db in range(4):
                wb, rw = wbuf()
                w2v = wb[:, 0:22 * 256].rearrange("p (k n) -> p k n", k=22)
                P.dma('pool', w2v, W['ffn_w2'][l][:, db * 256:(db + 1) * 256].rearrange("(k p) n -> p k n", p=128), wr=[rw])
                def w2_group(m, blk, sl):
                    n = sl.stop - sl.start
                    b, rb = P.bank()
                    P.MMG(b[:, 0:n], [(w2v[:, fc, m * 128:(m + 1) * 128], AT[:, fc, sl]) for fc in range(22)], [rw] + rAT[blk], [rb])
                    P.EV(MT[:, 2 * db + m, sl], b[:, 0:n], [rb], [rMT2[blk]])

                def w2_post(blk, sl):
                    n = sl.stop - sl.start
                    rstd_of(MT[:, :, sl], 8, [rMT2[blk]], onesd[:], n)
                    for fc in range(8):
                        P.TT('dve', TMPF[:, fc % 2, 0:n], MT[:, fc, sl], RS[:, 0:n], ALU.mult, [rMT2[blk], rRS], [rTMPF[fc % 2]])
                        P.STT('pool', XT[:, fc, sl], TMPF[:, fc % 2, 0:n], GG[:, l, 3, fc, v:v + 1], XT[:, fc, sl], ALU.mult, ALU.add,
                              [rTMPF[fc % 2], rGGw[l][3], rXT[blk]], [rXT[blk]])

                if db < 3:
                    for m in range(2):
                        for blk, sl in blks:
                            w2_group(m, blk, sl)
                else:
                    for blk, sl in blks:
                        for m in range(2):
                            w2_group(m, blk, sl)
                        w2_post(blk, sl)

        P.tag = "OUT" + gname + (str(l) if "l" in dir() else "")
        aTT.next_phase()
        rXS = [aTT.res("XS0"), aTT.res("XS1")]
        own_out = (not is_p) and CFG.get("own", True)
        if own_out:
            y_d = ysq_d
        for tt in range(2 if own_out else 8):
            k = tt % 2
            for half in range(2):
                b, rb = P.bank()
                for j in range(4):
                    P.TR(b[:, j * 128:(j + 1) * 128], XT[:, half * 4 + j, tt * 128:(tt + 1) * 128], ident[:],
                         [rXT[tt // 4], rC], [rb], inc=(j == 3))
                P.EV(XS[:, k, half * 512:(half + 1) * 512], b[:, :], [rb], [rXS[k]])
            P.dma('sp', y_d[tt * 128:(tt + 1) * 128, :], XS[:, k, :], rd=[rXS[k]], is_out=True)

    for g in CFG["groups"]:
        run_group(g)

    with nc.allow_non_contiguous_dma(reason="small strided parameter loads"):
        P.emit()
    return nc, P


_CACHE = {}


def kernel(**inputs):
    n = 8
    inp = {k: np.ascontiguousarray(np.asarray(v, dtype=np.float32)) for k, v in inputs.items()}
    consts = host_consts()
    if "nc" not in _CACHE:
        _CACHE["nc"] = build()
    nc, P = _CACHE["nc"]
    in_maps = []
    for r in range(n):
        b = r // 4
        m = {
            "xp": np.ascontiguousarray(inp["x_prompt"][4 * r:4 * r + 4].reshape(NTOK, D)),
            "xs": np.ascontiguousarray(inp["x_sample"][b]),
            "s0": np.ascontiguousarray(inp["state_gla"][b]),
            "cv": np.ascontiguousarray(np.stack([inp["c_ctx"], inp["c"][b]], axis=0)),
            "qoff": np.array([[256 * (r % 4)]], np.int32),
        }
        for k in WEIGHT_SHAPES:
            m[k] = inp[k]
        for k, vv in consts.items():
            m["c_" + k] = vv
        in_maps.append(m)
    res = run_bass_kernel_spmd(nc, in_maps, core_ids=list(range(n)))
    y_prompt = np.zeros((32, 256, D), np.float32)
    y_sample = np.zeros((2, 1024, D), np.float32)
    new_state = np.zeros((32, 2, 2, 4, 32, 64), np.float32)
    for r in range(n):
        o = res.results[r]
        y_prompt[4 * r:4 * r + 4] = np.asarray(o["yp"]).reshape(4, 256, D)
        new_state[4 * r:4 * r + 4] = np.asarray(o["ns"]).reshape(4, 2, 2, 4, 32, 64)
        if CFG.get("own", True):
            y_sample[r // 4, 256 * (r % 4):256 * (r % 4) + 256] = np.asarray(o["ysq"])
        elif r % 4 == 0:
            y_sample[r // 4] = np.asarray(o["ys"])
    return (y_prompt, y_sample, new_state)
```
